# Optimizing a Trainium2 kernel written in Bass

```python
import math
import jax, jax.numpy as jnp
from jax import lax
import numpy as np

D_MODEL = 1024
BATCH = 8
SEQ = 4096
DEPTH = 1

SSM_GROUP = 16
SSM_GROUPS = 32
SSM_WIDTH = SSM_GROUP * SSM_GROUPS
SSM_STATE = 64
SSM_DT_MIN = 0.001
SSM_DT_MAX = 0.1
MLA_HEADS = 8
MLA_QK_NOPE = 64
MLA_QK_ROPE = 32
MLA_V = 64
MLA_Q_LORA = 256
MLA_KV_LORA = 128
MLA_WIDTH = MLA_HEADS * MLA_V
ROPE_THETA = 10000.0
Q_BLOCK = 128
N_MEM = 256
MEM_HEADS = 4
MEM_HEAD_DIM = 128
MEM_WIDTH = MEM_HEADS * MEM_HEAD_DIM
N_BRANCH = 3
NORM_EPS = 1e-6
IN_SPLITS = (SSM_WIDTH, SSM_WIDTH, MLA_Q_LORA, MLA_KV_LORA, MLA_QK_ROPE, MLA_WIDTH,
             MEM_WIDTH, MEM_WIDTH, N_BRANCH * D_MODEL)
IN_WIDTH = 6048

kernel_name = 'hybrid_s5_mla_memory_gated_block'


def rms_norm(x, g):
    xf = x.astype(jnp.float32)
    xf = xf * lax.rsqrt(jnp.mean(xf * xf, axis=-1, keepdims=True) + NORM_EPS)
    return xf.astype(x.dtype) * g


def split_columns(t):
    parts, start = [], 0
    for w in IN_SPLITS:
        parts.append(t[..., start:start + w])
        start += w
    return parts


def rope_tables(positions):
    inv_freq = ROPE_THETA ** (-jnp.arange(0, MLA_QK_ROPE, 2, dtype=jnp.float32) / MLA_QK_ROPE)
    ang = positions.astype(jnp.float32)[..., None] * inv_freq
    return jnp.cos(ang), jnp.sin(ang)


def apply_rope(t, cos, sin):
    cos = cos.astype(t.dtype)
    sin = sin.astype(t.dtype)
    t1, t2 = jnp.split(t, 2, axis=-1)
    return jnp.concatenate([t1 * cos - t2 * sin, t2 * cos + t1 * sin], axis=-1)


def _complex_linear_combine(e1, e2):
    a1r, a1i, b1r, b1i = e1
    a2r, a2i, b2r, b2i = e2
    ar = a2r * a1r - a2i * a1i
    ai = a2r * a1i + a2i * a1r
    br = a2r * b1r - a2i * b1i + b2r
    bi = a2r * b1i + a2i * b1r + b2i
    return (ar, ai, br, bi)


def s5_direction(u_g, lam_re, lam_im, log_dt, b_re, b_im, c_re, c_im, reverse):
    seq_len = u_g.shape[1]
    dt = jnp.exp(log_dt)[:, None]
    decay = jnp.exp(lam_re * dt)
    a_re = decay * jnp.cos(lam_im * dt)
    a_im = decay * jnp.sin(lam_im * dt)
    denom = lam_re * lam_re + lam_im * lam_im
    f_re = a_re - 1.0
    z_re = (f_re * lam_re + a_im * lam_im) / denom
    z_im = (a_im * lam_re - f_re * lam_im) / denom
    bb_re = z_re[..., None] * b_re - z_im[..., None] * b_im
    bb_im = z_re[..., None] * b_im + z_im[..., None] * b_re
    bu_re = jnp.einsum('blgp,gnp->blgn', u_g, bb_re)
    bu_im = jnp.einsum('blgp,gnp->blgn', u_g, bb_im)
    a_re_t = jnp.broadcast_to(a_re[None, None], (1, seq_len) + a_re.shape)
    a_im_t = jnp.broadcast_to(a_im[None, None], (1, seq_len) + a_im.shape)
    _, _, s_re, s_im = lax.associative_scan(
        _complex_linear_combine, (a_re_t, a_im_t, bu_re, bu_im), reverse=reverse, axis=1)
    return jnp.einsum('blgn,gpn->blgp', s_re, c_re) - jnp.einsum('blgn,gpn->blgp', s_im, c_im)


def s5_branch(u, lam_re, lam_im, log_dt, b_re, b_im, c_re, c_im, d_skip, glu_w, glu_b):
    bsz, seq_len, _ = u.shape
    u_g = u.reshape(bsz, seq_len, SSM_GROUPS, SSM_GROUP)
    y = d_skip * u
    for direction in range(2):
        y = y + s5_direction(u_g, lam_re[direction], lam_im[direction], log_dt[direction],
                             b_re[direction], b_im[direction], c_re[direction], c_im[direction],
                             reverse=(direction == 1)).reshape(bsz, seq_len, SSM_WIDTH)
    y = jax.nn.gelu(y)
    return y * jax.nn.sigmoid(y @ glu_w + glu_b)


def mla_branch(c_q, c_kv, k_rope, cos, sin, q_norm, w_q_up, kv_norm, w_kv_up):
    bsz, seq_len, _ = c_q.shape
    q = (rms_norm(c_q, q_norm) @ w_q_up).reshape(bsz, seq_len, MLA_HEADS, MLA_QK_NOPE + MLA_QK_ROPE)
    q_nope = q[..., :MLA_QK_NOPE]
    q_rope = apply_rope(q[..., MLA_QK_NOPE:], cos[:, :, None, :], sin[:, :, None, :])
    kv = (rms_norm(c_kv, kv_norm) @ w_kv_up).reshape(bsz, seq_len, MLA_HEADS, MLA_QK_NOPE + MLA_V)
    k_nope = kv[..., :MLA_QK_NOPE]
    v = kv[..., MLA_QK_NOPE:]
    k_rope = apply_rope(k_rope, cos, sin)
    scale = (MLA_QK_NOPE + MLA_QK_ROPE) ** -0.5
    n_blk = seq_len // Q_BLOCK

    def blocks(t):
        return t.reshape((bsz, n_blk, Q_BLOCK) + t.shape[2:]).swapaxes(0, 1)

    def attend(qb):
        qn, qr = qb
        s = jnp.einsum('bqhd,bkhd->bhqk', qn, k_nope) + jnp.einsum('bqhr,bkr->bhqk', qr, k_rope)
        p = jax.nn.softmax(s.astype(jnp.float32) * scale, axis=-1).astype(v.dtype)
        return jnp.einsum('bhqk,bkhd->bqhd', p, v)

    o = lax.map(attend, (blocks(q_nope), blocks(q_rope)))
    return o.swapaxes(0, 1).reshape(bsz, seq_len, MLA_WIDTH)


def memory_branch(q_mem, mem, mem_norm, mem_w_kv):
    bsz, seq_len, _ = q_mem.shape
    n_mem = mem.shape[1]
    kv = (rms_norm(mem, mem_norm) @ mem_w_kv).reshape(bsz, n_mem, 2, MEM_HEADS, MEM_HEAD_DIM)
    k, v = kv[:, :, 0], kv[:, :, 1]
    q = q_mem.reshape(bsz, seq_len, MEM_HEADS, MEM_HEAD_DIM)
    s = jnp.einsum('blhd,bmhd->bhlm', q, k).astype(jnp.float32) * (MEM_HEAD_DIM ** -0.5)
    p = jax.nn.softmax(s, axis=-1).astype(v.dtype)
    return jnp.einsum('bhlm,bmhd->blhd', p, v).reshape(bsz, seq_len, MEM_WIDTH)


def setup_inputs(seed: int = 0) -> dict:
    key = jax.random.key(seed)
    ks = jax.random.split(key, 32)
    f32 = jnp.float32

    def nrm(k, shape, scale):
        return jax.random.normal(k, shape, f32) * scale

    nl = DEPTH
    G, N, P = SSM_GROUPS, SSM_STATE, SSM_GROUP
    x = nrm(ks[0], (BATCH, SEQ, D_MODEL), 1.0)
    mem = nrm(ks[1], (BATCH, N_MEM, D_MODEL), 1.0)
    positions = (jnp.arange(SEQ, dtype=jnp.int32)[None, :]
                 + jax.random.randint(ks[2], (BATCH, 1), 0, 1024, dtype=jnp.int32))
    pre_norm = 1.0 + nrm(ks[3], (nl, D_MODEL), 0.05)
    w_in = nrm(ks[4], (nl, D_MODEL, IN_WIDTH), D_MODEL ** -0.5)
    b_gate = nrm(ks[5], (nl, N_BRANCH * D_MODEL), 0.01)
    ssm_lambda_re = -0.5 + nrm(ks[6], (nl, 2, G, N), 0.01)
    ssm_lambda_im = math.pi * jnp.arange(N, dtype=f32) + nrm(ks[7], (nl, 2, G, N), 0.01)
    ssm_log_dt = jax.random.uniform(ks[8], (nl, 2, G), f32,
                                    math.log(SSM_DT_MIN), math.log(SSM_DT_MAX))
    ssm_b_re = nrm(ks[9], (nl, 2, G, N, P), (2 * P) ** -0.5)
    ssm_b_im = nrm(ks[10], (nl, 2, G, N, P), (2 * P) ** -0.5)
    ssm_c_re = nrm(ks[11], (nl, 2, G, P, N), N ** -0.5)
    ssm_c_im = nrm(ks[12], (nl, 2, G, P, N), N ** -0.5)
    ssm_d = nrm(ks[13], (nl, SSM_WIDTH), 1.0)
    ssm_glu_w = nrm(ks[14], (nl, SSM_WIDTH, SSM_WIDTH), SSM_WIDTH ** -0.5)
    ssm_glu_b = nrm(ks[15], (nl, SSM_WIDTH), 0.01)
    mla_q_norm = 1.0 + nrm(ks[16], (nl, MLA_Q_LORA), 0.05)
    mla_w_q_up = nrm(ks[17], (nl, MLA_Q_LORA, MLA_HEADS * (MLA_QK_NOPE + MLA_QK_ROPE)), MLA_Q_LORA ** -0.5)
    mla_kv_norm = 1.0 + nrm(ks[18], (nl, MLA_KV_LORA), 0.05)
    mla_w_kv_up = nrm(ks[19], (nl, MLA_KV_LORA, MLA_HEADS * (MLA_QK_NOPE + MLA_V)), MLA_KV_LORA ** -0.5)
    mem_norm = 1.0 + nrm(ks[20], (nl, D_MODEL), 0.05)
    mem_w_kv = nrm(ks[21], (nl, D_MODEL, 2 * MEM_WIDTH), D_MODEL ** -0.5)
    w_branch_ssm = nrm(ks[22], (nl, SSM_WIDTH, D_MODEL), SSM_WIDTH ** -0.5)
    w_branch_mla = nrm(ks[23], (nl, MLA_WIDTH, D_MODEL), MLA_WIDTH ** -0.5)
    w_branch_mem = nrm(ks[24], (nl, MEM_WIDTH, D_MODEL), MEM_WIDTH ** -0.5)
    w_out = nrm(ks[25], (nl, D_MODEL, D_MODEL), D_MODEL ** -0.5)
    post_norm = 1.0 + nrm(ks[26], (nl, D_MODEL), 0.05)
    return {'x': x, 'mem': mem, 'positions': positions, 'pre_norm': pre_norm, 'w_in': w_in,
            'b_gate': b_gate, 'ssm_lambda_re': ssm_lambda_re, 'ssm_lambda_im': ssm_lambda_im,
            'ssm_log_dt': ssm_log_dt, 'ssm_b_re': ssm_b_re, 'ssm_b_im': ssm_b_im,
            'ssm_c_re': ssm_c_re, 'ssm_c_im': ssm_c_im, 'ssm_d': ssm_d, 'ssm_glu_w': ssm_glu_w,
            'ssm_glu_b': ssm_glu_b, 'mla_q_norm': mla_q_norm, 'mla_w_q_up': mla_w_q_up,
            'mla_kv_norm': mla_kv_norm, 'mla_w_kv_up': mla_w_kv_up, 'mem_norm': mem_norm,
            'mem_w_kv': mem_w_kv, 'w_branch_ssm': w_branch_ssm, 'w_branch_mla': w_branch_mla,
            'w_branch_mem': w_branch_mem, 'w_out': w_out, 'post_norm': post_norm}


def reference(x, mem, positions, pre_norm, w_in, b_gate, ssm_lambda_re, ssm_lambda_im,
              ssm_log_dt, ssm_b_re, ssm_b_im, ssm_c_re, ssm_c_im, ssm_d, ssm_glu_w,
              ssm_glu_b, mla_q_norm, mla_w_q_up, mla_kv_norm, mla_w_kv_up, mem_norm,
              mem_w_kv, w_branch_ssm, w_branch_mla, w_branch_mem, w_out, post_norm):
    bsz, seq_len, _ = x.shape
    cos, sin = rope_tables(positions)
    for l in range(DEPTH):
        h = rms_norm(x, pre_norm[l])
        proj = jnp.einsum('bld,dc->blc', h, w_in[l])
        u_ssm, z_ssm, c_q, c_kv, k_rope, z_mla, q_mem, z_mem, gate_logits = split_columns(proj)
        y_ssm = s5_branch(u_ssm, ssm_lambda_re[l], ssm_lambda_im[l], ssm_log_dt[l],
                          ssm_b_re[l], ssm_b_im[l], ssm_c_re[l], ssm_c_im[l],
                          ssm_d[l], ssm_glu_w[l], ssm_glu_b[l])
        y_mla = mla_branch(c_q, c_kv, k_rope, cos, sin, mla_q_norm[l], mla_w_q_up[l],
                           mla_kv_norm[l], mla_w_kv_up[l])
        y_mem = memory_branch(q_mem, mem, mem_norm[l], mem_w_kv[l])
        gates = jax.nn.sigmoid(gate_logits + b_gate[l]).reshape(bsz, seq_len, N_BRANCH, D_MODEL)
        merged = (gates[:, :, 0] * ((jax.nn.silu(z_ssm) * y_ssm) @ w_branch_ssm[l])
                  + gates[:, :, 1] * ((jax.nn.silu(z_mla) * y_mla) @ w_branch_mla[l])
                  + gates[:, :, 2] * ((jax.nn.silu(z_mem) * y_mem) @ w_branch_mem[l]))
        out = merged @ w_out[l]
        x = x + rms_norm(out, post_norm[l])
    return x
```

```python
import math
import numpy as np
from contextlib import ExitStack
import concourse.bass as bass
import concourse.mybir as mybir
from concourse.bass_utils import run_bass_kernel_spmd
from concourse.ap import AP as RawAP

F32 = mybir.dt.float32
BF16 = mybir.dt.bfloat16
I32 = mybir.dt.int32
AF = mybir.ActivationFunctionType
ALU = mybir.AluOpType

L = 4096
D = 1024
NT = L // 128
NQ = L // 512
EPS = 1e-6
TWO_PI = 2.0 * math.pi
C1 = 6.28125
C2 = TWO_PI - C1
O_U, O_ZS, O_CQ, O_CKV, O_KR, O_ZM, O_QM, O_ZME, O_G = 0, 512, 1024, 1280, 1408, 1440, 1952, 2464, 2976


class Tok:
    __slots__ = ("w", "r")

    def __init__(self):
        self.w = None
        self.r = {}


class KB:
    def __init__(self, nc):
        self.nc = nc
        self.es = ExitStack()
        self.E = {"pe": nc.tensor, "act": nc.scalar, "dve": nc.vector, "pool": nc.gpsimd, "sp": nc.sync}
        self.sem = {}
        self.cnt = {}
        self.waited = {e: {} for e in self.E}
        for e in self.E:
            self.sem[e] = self.es.enter_context(nc.semaphore("sem_" + e))
            self.cnt[e] = 0
        self.dsem = {}
        for q, n in (("sp", 24), ("pool", 8)):
            self.dsem[q] = [[self.es.enter_context(nc.semaphore("d%s%d" % (q, i))), 0] for i in range(n)]
        self.dnext = {"sp": 0, "pool": 0}
        self.toks = {}
        self.same_engine_sync = True

    def tok(self, name):
        t = self.toks.get(name)
        if t is None:
            t = Tok()
            self.toks[name] = t
        return t

    def T(self, names):
        if isinstance(names, str):
            names = [names]
        return [self.tok(n) if isinstance(n, str) else n for n in names]

    def wait(self, eng, ev):
        sem, val = ev
        if sem is self.sem[eng]:
            if eng in ("pe", "sp") or not self.same_engine_sync:
                return
        w = self.waited[eng]
        if w.get(sem.num, 0) >= val:
            return
        self.E[eng].wait_ge(sem, val)
        w[sem.num] = val

    def deps(self, eng, R, W):
        for t in R:
            if t.w is not None:
                self.wait(eng, t.w)
        for t in W:
            for ev in t.r.values():
                self.wait(eng, ev)
            if t.w is not None:
                self.wait(eng, t.w)

    def mark(self, R, W, ev):
        for t in R:
            t.r[ev[0].num] = ev
        for t in W:
            t.w = ev
            t.r = {}

    def op(self, eng, R, W, fn):
        R = self.T(R)
        W = self.T(W)
        self.deps(eng, R, W)
        inst = fn(self.E[eng])
        self.cnt[eng] += 1
        inst.then_inc(self.sem[eng], 1)
        self.mark(R, W, (self.sem[eng], self.cnt[eng]))

    def dma(self, q, R, W, out, in_, **kw):
        R = self.T(R)
        W = self.T(W)
        self.deps(q, R, W)
        pool = self.dsem[q]
        i = self.dnext[q]
        self.dnext[q] = (i + 1) % len(pool)
        sem, c = pool[i]
        if c > 0:
            self.wait(q, (sem, c))
        self.E[q].dma_start(out=out, in_=in_, **kw).then_inc(sem, 16)
        pool[i][1] = c + 16
        self.mark(R, W, (sem, c + 16))

    def barrier(self):
        evs = [(self.sem[e], self.cnt[e]) for e in self.E if self.cnt[e] > 0]
        for q in self.dsem:
            for sem, c in self.dsem[q]:
                if c > 0:
                    evs.append((sem, c))
        for e in self.E:
            for ev in evs:
                self.wait(e, ev)

    def sb(self, es, name, shape, dt):
        return es.enter_context(self.nc.sbuf_tensor(name, list(shape), dt))

    def ps(self, es, name, shape, dt):
        return es.enter_context(self.nc.psum_tensor(name, list(shape), dt))


def build(debug=None, phases="0AMSC"):
    nc = bass.Bass("TRN2", target_bir_lowering=False)
    kb = KB(nc)

    def din(name, shape, dt=F32):
        return nc.dram_tensor(name, list(shape), dt, kind="ExternalInput").ap()

    x = din("x", [L, D])
    mem = din("mem", [256, D])
    pos = din("pos", [1, L], I32)
    pre_norm = din("pre_norm", [1, D])
    w_in = din("w_in", [D, 6048])
    b_gate = din("b_gate", [1, 3072])
    lam_re = din("lam_re", [2, 32, 64])
    lam_im = din("lam_im", [2, 32, 64])
    log_dt = din("log_dt", [2, 32])
    b_re = din("b_re", [2, 32, 64, 16])
    b_im = din("b_im", [2, 32, 64, 16])
    c_re = din("c_re", [2, 512, 64])
    c_im = din("c_im", [2, 512, 64])
    ssm_d = din("ssm_d", [1, 512])
    glu_w = din("glu_w", [512, 512])
    glu_b = din("glu_b", [1, 512])
    q_norm = din("q_norm", [1, 256])
    w_q_up = din("w_q_up", [256, 768])
    kv_norm = din("kv_norm", [1, 128])
    w_kv_up = din("w_kv_up", [128, 1024])
    mem_norm = din("mem_norm", [1, D])
    mem_w_kv = din("mem_w_kv", [D, 1024])
    w_br = [din("w_br%d" % i, [512, D]) for i in range(3)]
    w_out = din("w_out", [D, D])
    post_norm = din("post_norm", [1, D])
    cident = din("cident", [128, 128])
    cjm = din("cjm", [128, 128])
    cbd = din("cbd", [128, 128])
    ccol = din("ccol", [128, 8])
    out = nc.dram_tensor("out", [L, D], F32, kind="ExternalOutput").ap()
    wsc = nc.dram_tensor("wsc", [40, 128, 8, 128], BF16, kind="Internal").ap()
    wbsc = nc.dram_tensor("wbsc", [3, 128, 4, D], BF16, kind="Internal").ap()
    wosc = nc.dram_tensor("wosc", [128, 8, D], BF16, kind="Internal").ap()
    dbg = {}
    if debug:
        for name, shape in debug.items():
            dbg[name] = nc.dram_tensor("dbg_" + name, list(shape), F32, kind="ExternalOutput").ap()

    op, dma = kb.op, kb.dma
    es0 = kb.es
    w_in_v = w_in.rearrange("(k p) c -> p k c", p=128)

    def col_view(v, n):
        return v.rearrange("o (k p) -> p (o k)", p=128)

    with es0:
        ident_f = kb.sb(es0, "ident_f", [128, 128], F32)
        ident_b = kb.sb(es0, "ident_b", [128, 128], BF16)
        ones_b = kb.sb(es0, "ones_b", [128, 128], BF16)
        ccols = kb.sb(es0, "ccols", [128, 8], F32)
        gpre = kb.sb(es0, "gpre", [128, 8], F32)
        uT = kb.sb(es0, "uT", [128, 4, L], BF16)
        ymlaT = kb.sb(es0, "ymlaT", [128, 4, L], BF16)
        kmemT = kb.sb(es0, "kmemT", [128, 4, 256], BF16)
        vmem = kb.sb(es0, "vmem", [128, 2, 512], BF16)
        PSP = [kb.ps(es0, "psp%d" % i, [128, 1024], F32) for i in range(4)]
        PS = [PSP[i // 2][:, (i % 2) * 512:(i % 2 + 1) * 512] for i in range(8)]
        PSB = PS[7][:, :].bitcast(BF16)
        psrr = [0]

        def nps():
            i = psrr[0]
            psrr[0] = (i + 1) % 4
            return PS[i], "PS%d" % i

        evrr = [0]

        def evac_eng():
            evrr[0] ^= 1
            return "dve" if evrr[0] else "act"

        def copy_op(eng, R, W, out_ap, in_ap):
            if eng == "act":
                op("act", R, W, lambda e: e.copy(out=out_ap, in_=in_ap))
            else:
                op(eng, R, W, lambda e: e.tensor_copy(out=out_ap, in_=in_ap))

        def dump(name, R, src_ap, dst_ap=None):
            if name in dbg:
                dma("sp" if src_ap.dtype == F32 else "pool", R, ["dbgout"], dbg[name] if dst_ap is None else dst_ap, src_ap)

        def rstd_from_sum(es, tag, ssum_ap, R, n, dim, Wname):
            shape = list(ssum_ap.shape)
            t = kb.sb(es, "rs_" + tag, shape, F32)
            op("dve", R, [Wname], lambda e: e.tensor_scalar(out=t[:], in0=ssum_ap, scalar1=1.0 / dim, scalar2=EPS, op0=ALU.mult, op1=ALU.add))
            op("act", [Wname], [Wname], lambda e: e.activation(out=t[:], in_=t[:], func=AF.Sqrt))
            op("dve", [Wname], [Wname], lambda e: e.reciprocal(out=t[:], in_=t[:]))
            return t

        dma("sp", [], ["ident_f"], ident_f[:], cident)
        dma("pool", [], ["ident_b"], ident_b[:], cident)
        dma("sp", [], ["ccols"], ccols[:], ccol)
        dma("sp", [], ["gpre"], gpre[:], col_view(pre_norm, 8), allow_slow_non_contiguous=True)
        op("dve", [], ["ones_b"], lambda e: e.memset(ones_b[:], 1.0))

        esT = ExitStack()
        ssm_T = ssm_tables_alloc(kb, esT)
        esA = ExitStack()
        cosT = kb.sb(esA, "cosT", [128, L // 4], F32)
        sinT = kb.sb(esA, "sinT", [128, L // 4], F32)

        def rope_tab(tab, c):
            blk = c // 2
            return tab[32 * blk:32 * blk + 32, (c % 2) * 512:(c % 2) * 512 + 512]

        NA = 1088
        wa = kb.sb(esA, "wa", [128, 8, NA], BF16)
        gq = kb.sb(esA, "gq", [128, 2], F32)
        gkv = kb.sb(esA, "gkv", [128, 1], F32)
        wq = kb.sb(esA, "wq", [128, 2, 8, 128], BF16)
        wkv2 = kb.sb(esA, "wkv2", [128, 1024], BF16)
        wqv = w_q_up.rearrange("(k p) (h c) -> p k h c", p=128, c=96)

        def issue_early_weight_loads():
            dma("pool", [], ["wa"], wa[:, :, 0:512], w_in_v[:, :, O_U:O_U + 512])
            dma("pool", [], ["wa"], wa[:, :, 512:768], w_in_v[:, :, O_CQ:O_CQ + 256])
            dma("pool", [], ["wa"], wa[:, :, 768:896], w_in_v[:, :, O_CKV:O_CKV + 128])
            dma("pool", [], ["wa"], wa[:, :, 896:960], w_in_v[:, :, O_CKV:O_CKV + 64])
            dma("pool", [], ["wa"], wa[:, :, 960:992], w_in_v[:, :, O_KR:O_KR + 32])
            dma("pool", [], ["wa"], wa[:, :, 992:1056], w_in_v[:, :, O_CKV:O_CKV + 64])
            dma("pool", [], ["wa"], wa[:, :, 1056:1072], w_in_v[:, :, O_KR + 16:O_KR + 32])
            dma("pool", [], ["wa"], wa[:, :, 1072:1088], w_in_v[:, :, O_KR:O_KR + 16])
            dma("sp", [], ["gq"], gq[:], col_view(q_norm, 2), allow_slow_non_contiguous=True)
            dma("sp", [], ["gkv"], gkv[:], col_view(kv_norm, 1), allow_slow_non_contiguous=True)
            for k in range(2):
                dma("pool", [], ["wq"], wq[:, k, :, 0:96], wqv[:, k, :, :])
                dma("pool", [], ["wq"], wq[:, k, :, 96:112], wqv[:, k, :, 80:96])
                dma("pool", [], ["wq"], wq[:, k, :, 112:128], wqv[:, k, :, 64:80])
            dma("pool", [], ["wkv2"], wkv2[:], w_kv_up)

        with ExitStack() as es:
            ssm_tables_ = ssm_tables(kb, ssm_T, es, locals())
            RC = L // 4
            pos_i = kb.sb(es, "pos_i", [128, RC], I32)
            ang = kb.sb(es, "ang", [128, RC], F32)
            t1 = kb.sb(es, "t1", [128, RC], F32)
            ki = kb.sb(es, "ki", [128, RC], I32)
            for blk in range(4):
                dma("sp", [], ["pos_i"], pos_i[32 * blk:32 * blk + 32, :], pos[:, blk * RC:(blk + 1) * RC].partition_broadcast(32))
            op("dve", ["pos_i"], ["ang"], lambda e: e.tensor_copy(out=ang[:], in_=pos_i[:]))
            op("dve", ["ang", "ccols"], ["ang"], lambda e: e.tensor_scalar(out=ang[:], in0=ang[:], scalar1=ccols[:, 0:1], scalar2=None, op0=ALU.mult))
            for which, tab, tn, scale_ap in (("sin", sinT, "sinT", ccols[:, 1:2]), ("cos", cosT, "cosT", 1.0)):
                if which == "cos":
                    op("dve", ["ang"], ["ang"], lambda e: e.tensor_scalar(out=ang[:], in0=ang[:], scalar1=math.pi / 2, scalar2=None, op0=ALU.add))
                op("dve", ["ang"], ["t1"], lambda e: e.tensor_scalar(out=t1[:], in0=ang[:], scalar1=1.0 / TWO_PI, scalar2=None, op0=ALU.mult))
                op("dve", ["t1"], ["ki"], lambda e: e.tensor_copy(out=ki[:], in_=t1[:]))
                op("dve", ["ki"], ["t1"], lambda e: e.tensor_copy(out=t1[:], in_=ki[:]))
                op("dve", ["t1", "ang"], [tn], lambda e: e.scalar_tensor_tensor(out=tab[:], in0=t1[:], scalar=-C1, in1=ang[:], op0=ALU.mult, op1=ALU.add))
                op("dve", ["t1", tn], [tn], lambda e: e.scalar_tensor_tensor(out=tab[:], in0=t1[:], scalar=-C2, in1=tab[:], op0=ALU.mult, op1=ALU.add))
                op("dve", [tn], [tn], lambda e: e.tensor_scalar(out=tab[:], in0=tab[:], scalar1=-math.pi, scalar2=math.pi, op0=ALU.max, op1=ALU.min))
                op("act", [tn, "ccols"], [tn], lambda e: e.activation(out=tab[:], in_=tab[:], func=AF.Sin, scale=scale_ap))
            dump("cosT", ["cosT"], cosT[:])
            dump("sinT", ["sinT"], sinT[:])

            memt = kb.sb(es, "memt", [128, 2, D], F32)
            memn = kb.sb(es, "memn", [128, 2, D], BF16)
            gmem = kb.sb(es, "gmem", [128, D], F32)
            memnT = kb.sb(es, "memnT", [128, 8, 256], BF16)
            wkv = kb.sb(es, "wkv", [128, 8, 1024], BF16)
            junk = kb.sb(es, "junk", [128, D], F32)
            ssm_ = kb.sb(es, "ssm_", [128, 2], F32)
            dma("sp", [], ["memt"], memt[:], mem.rearrange("(t p) d -> p t d", p=128))
            dma("sp", [], ["gmem"], gmem[:], mem_norm.partition_broadcast(128))
            dma("pool", [], ["wkv"], wkv[:], mem_w_kv.rearrange("(k p) c -> p k c", p=128))
            issue_early_weight_loads()
            for t in range(2):
                op("act", ["memt"], ["junk", "ssm_"], lambda e: e.activation(out=junk[:], in_=memt[:, t, :], func=AF.Square, accum_out=ssm_[:, t:t + 1]))
            rs = rstd_from_sum(es, "mem", ssm_[:], ["ssm_"], 128, D, "rs_mem")
            for t in range(2):
                op("dve", ["memt", "rs_mem", "gmem"], ["memn"], lambda e: e.scalar_tensor_tensor(out=memn[:, t, :], in0=memt[:, t, :], scalar=rs[:, t:t + 1], in1=gmem[:], op0=ALU.mult, op1=ALU.mult))
            for t in range(2):
                for k in range(8):
                    op("pe", ["memn", "ident_b"], ["PSB"], lambda e: e.transpose(PSB[:, k * 128:(k + 1) * 128], memn[:, t, k * 128:(k + 1) * 128], ident_b[:]))
                op("dve", ["PSB"], ["memnT"], lambda e: e.tensor_copy(out=memnT[:, :, t * 128:(t + 1) * 128], in_=PSB.rearrange("p (k c) -> p k c", k=8)))
            for h in range(4):
                p_, pn = nps()

                def f(e):
                    for k in range(8):
                        i_ = e.matmul(p_[:, 0:256], lhsT=wkv[:, k, h * 128:(h + 1) * 128], rhs=memnT[:, k, :], start=(k == 0), stop=(k == 7))
                    return i_
                op("pe", ["wkv", "memnT"], [pn], f)
                op("dve", [pn], ["kmemT"], lambda e: e.tensor_copy(out=kmemT[:, h, :], in_=p_[:, 0:256]))
            for t in range(2):
                p_, pn = nps()

                def f(e):
                    for k in range(8):
                        i_ = e.matmul(p_[:, :], lhsT=memnT[:, k, t * 128:(t + 1) * 128], rhs=wkv[:, k, 512:1024], start=(k == 0), stop=(k == 7))
                    return i_
                op("pe", ["wkv", "memnT"], [pn], f)
                op("dve", [pn], ["vmem"], lambda e: e.tensor_copy(out=vmem[:, t, :], in_=p_[:, :]))
            dump("kmemT", ["kmemT"], kmemT[:])
            dump("vmem", ["vmem"], vmem[:])
            kb.barrier()

        cqnT = kb.sb(esA, "cqnT", [128, 2, L], BF16)
        ckvnT = kb.sb(esA, "ckvnT", [128, L], BF16)
        krT = kb.sb(esA, "krT", [128, L], BF16)

        def hT_norm(c, i, xts, hbs, ssx, junk, junkn):
            ti = c * 4 + i
            xt, xn = xts[ti % len(xts)], "xt%d" % (ti % len(xts))
            hb, hbn = hbs[ti % len(hbs)], "hb%d" % (ti % len(hbs))
            sl = ti % len(hbs)
            sx, sxn = ssx[:, sl, :], "ssx%d" % sl
            dma("sp", [], [xn], xt[:], x[ti * 128:(ti + 1) * 128, :])
            op("act", [xn], [junkn, sxn], lambda e: e.activation(out=junk[:], in_=xt[:], func=AF.Square, accum_out=sx[:, 0:1]))
            op("dve", [sxn], [sxn], lambda e: e.tensor_scalar(out=sx[:, 1:2], in0=sx[:, 0:1], scalar1=1.0 / D, scalar2=EPS, op0=ALU.mult, op1=ALU.add))
            op("act", [sxn], [sxn], lambda e: e.activation(out=sx[:, 2:3], in_=sx[:, 1:2], func=AF.Sqrt))
            op("dve", [sxn], [sxn], lambda e: e.reciprocal(out=sx[:, 3:4], in_=sx[:, 2:3]))
            op("dve", [xn, sxn], [hbn], lambda e: e.tensor_scalar(out=hb[:], in0=xt[:], scalar1=sx[:, 3:4], scalar2=None, op0=ALU.mult))

        def hT_tr(c, i, hT, hn, hbs):
            ti = c * 4 + i
            hb, hbn = hbs[ti % len(hbs)], "hb%d" % (ti % len(hbs))

            def f(e):
                for k in range(8):
                    i_ = e.transpose(PSB[:, k * 128:(k + 1) * 128], hb[:, k * 128:(k + 1) * 128], ident_b[:])
                return i_
            op("pe", [hbn, "ident_b"], ["PSB"], f)
            op("dve", ["PSB", "gpre"], [hn], lambda e: e.tensor_tensor(out=hT[:, :, i * 128:(i + 1) * 128], in0=PSB.rearrange("p (k c) -> p k c", k=8), in1=gpre[:].unsqueeze(2).to_broadcast([128, 8, 128]), op=ALU.mult))

        if "A" in phases:
            with ExitStack() as es:
                xts = [kb.sb(es, "xt%d" % i, [128, D], F32) for i in range(2)]
                junkb = kb.sb(es, "junkb", [128, D], BF16)
                hbs = [kb.sb(es, "hb%d" % i, [128, D], BF16) for i in range(4)]
                ssx = kb.sb(es, "ssx", [128, 4, 4], F32)
                hTs = [kb.sb(es, "hT%d" % i, [128, 8, 512], BF16) for i in range(2)]
                for i in range(4):
                    hT_norm(0, i, xts, hbs, ssx, junkb, "junkb")
                sq = kb.sb(es, "sq", [128, 3, 512], BF16)
                epsc = kb.sb(es, "epsc", [128, 1], F32)
                op("dve", [], ["epsc"], lambda e: e.memset(epsc[:], EPS))
                rq = kb.sb(es, "rq", [128, 512], F32)
                rkv = kb.sb(es, "rkv", [128, 512], F32)
                tm1 = kb.sb(es, "tm1", [128, 512], F32)
                tm2 = kb.sb(es, "tm2", [128, 512], F32)
                for c in range(NQ):
                    hT = hTs[c % 2]
                    hn = "hT%d" % (c % 2)
                    cs = slice(c * 512, (c + 1) * 512)
                    if c == 0:
                        for i in range(4):
                            hT_tr(0, i, hT, hn, hbs)
                    if c + 1 < NQ:
                        for i in range(4):
                            hT_norm(c + 1, i, xts, hbs, ssx, junkb, "junkb")

                    def proj(p_, pn, c0, m):
                        def f(e):
                            for k in range(8):
                                i_ = e.matmul(p_[0:m, :], lhsT=wa[:, k, c0:c0 + m], rhs=hT[:, k, :], start=(k == 0), stop=(k == 7))
                            return i_
                        op("pe", ["wa", hn], [pn], f)
                    for m in range(4):
                        p_, pn = nps()
                        proj(p_, pn, m * 128, 128)
                        copy_op(evac_eng(), [pn], ["uT%d_%d" % (m, c)], uT[:, m, cs], p_[:, :])
                    for m in range(2):
                        proj(PS[4 + m], "PS%d" % (4 + m), 512 + m * 128, 128)
                    pkv, pkvn = nps()
                    proj(pkv, pkvn, 768, 128)
                    pka, pkan = nps()
                    proj(pka, pkan, 896, 96)
                    pkb, pkbn = nps()
                    proj(pkb, pkbn, 992, 96)
                    for m in range(2):
                        op("act", ["PS%d" % (4 + m)], ["sq%d" % m], lambda e: e.activation(out=sq[:, m, :], in_=PS[4 + m][:, :], func=AF.Square))
                    op("act", [pkvn], ["sq2"], lambda e: e.activation(out=sq[:, 2, :], in_=pkv[:, :], func=AF.Square))

                    def f(e):
                        for m in range(2):
                            i_ = e.matmul(PS[6][:, :], lhsT=ones_b[:], rhs=sq[:, m, :], start=(m == 0), stop=(m == 1))
                        return i_
                    op("pe", ["sq0", "sq1", "ones_b"], ["PS6"], f)
                    pss, pssn = nps()
                    op("pe", ["sq2", "ones_b"], [pssn], lambda e: e.matmul(pss[:, :], lhsT=ones_b[:], rhs=sq[:, 2, :], start=True, stop=True))
                    if c + 1 < NQ:
                        for i in range(4):
                            hT_tr(c + 1, i, hTs[(c + 1) % 2], "hT%d" % ((c + 1) % 2), hbs)
                    op("act", ["PS6", "epsc"], ["rq"], lambda e: e.activation(out=rq[:], in_=PS[6][:, :], func=AF.Ln, scale=1.0 / 256, bias=epsc[:, 0:1]))
                    op("act", [pssn, "epsc"], ["rkv"], lambda e: e.activation(out=rkv[:], in_=pss[:, :], func=AF.Ln, scale=1.0 / 128, bias=epsc[:, 0:1]))
                    op("act", ["rq"], ["rq"], lambda e: e.activation(out=rq[:], in_=rq[:], func=AF.Exp, scale=-0.5))
                    op("act", ["rkv"], ["rkv"], lambda e: e.activation(out=rkv[:], in_=rkv[:], func=AF.Exp, scale=-0.5))
                    for m in range(2):
                        op("dve", ["PS%d" % (4 + m), "rq", "gq"], ["cqnT%d" % c], lambda e: e.scalar_tensor_tensor(out=cqnT[:, m, cs], in0=PS[4 + m][:, :], scalar=gq[:, m:m + 1], in1=rq[:], op0=ALU.mult, op1=ALU.mult))
                    op("dve", [pkvn, "rkv", "gkv"], ["ckvnT%d" % c], lambda e: e.scalar_tensor_tensor(out=ckvnT[:, cs], in0=pkv[:, :], scalar=gkv[:, 0:1], in1=rkv[:], op0=ALU.mult, op1=ALU.mult))
                    op("dve", [pkan, "cosT"], ["tm1"], lambda e: e.tensor_tensor(out=tm1[64:96, :], in0=pka[64:96, :], in1=rope_tab(cosT, c), op=ALU.mult))
                    op("dve", [pkbn, "sinT"], ["tm2"], lambda e: e.tensor_tensor(out=tm2[64:96, :], in0=pkb[64:96, :], in1=rope_tab(sinT, c), op=ALU.mult))
                    op("dve", ["tm1", "tm2"], ["krT%d" % c], lambda e: e.tensor_tensor(out=krT[64:96, cs], in0=tm1[64:96, :], in1=tm2[64:96, :], op=ALU.add))
                allc = lambda nm: ["%s%d" % (nm, c) for c in range(NQ)]
                dump("uT", ["uT%d_%d" % (m, c) for m in range(4) for c in range(NQ)], uT[:])
                dump("cqnT", allc("cqnT"), cqnT[:])
                dump("ckvnT", allc("ckvnT"), ckvnT[:])
                dump("krT", allc("krT"), krT[64:96, :])
                kb.barrier()

        if "M" in phases:
            with ExitStack() as es:
                ccols_c = [O_ZS + m * 128 for m in range(4)] + [O_ZM + m * 128 for m in range(4)] + [O_QM + m * 128 for m in range(4)] + [O_ZME + m * 128 for m in range(4)]
                ccols_c += [O_G + b_ * 1024 + m * 128 for m in range(8) for b_ in range(3)]
                for mi, c0 in enumerate(ccols_c):
                    dma("pool", [], ["wsc"], wsc[mi], w_in_v[:, :, c0:c0 + 128])
                for b_ in range(3):
                    dma("pool", [], ["wbsc"], wbsc[b_], w_br[b_].rearrange("(k p) c -> p k c", p=128))
                dma("pool", [], ["wosc"], wosc, w_out.rearrange("(k p) c -> p k c", p=128))
                wkvv = wkv2[:].rearrange("p (h c) -> p h c", c=128)
                KT = [kb.sb(es, "KT%d" % i, [128, L], BF16) for i in range(2)]
                VA = [kb.sb(es, "VA%d" % i, [128, NT, 128], BF16) for i in range(2)]
                op("dve", [], ["VA0"], lambda e: e.memset(VA[0][:, :, 64:128], 1.0))
                op("dve", [], ["VA1"], lambda e: e.memset(VA[1][:, :, 0:64], 1.0))
                QT = [kb.sb(es, "QT%d" % i, [128, 512], BF16) for i in range(2)]
                tq1 = kb.sb(es, "tq1", [128, 512], F32)
                tq2 = kb.sb(es, "tq2", [128, 512], F32)
                rinv = kb.sb(es, "rinv", [128, 512], F32)
                scale = 96.0 ** -0.5
                allA = lambda nm: ["%s%d" % (nm, c) for c in range(NQ)]
                free_slots = [0, 1, 2]

                def next_slot():
                    return free_slots.pop(0)

                def release(j):
                    free_slots.append(j)
                PTp = [kb.sb(es, "PTp%d" % i, [128, 1024], BF16) for i in range(3)]

                def gen_kv_steps(h, bank, bn):
                    par = h % 2
                    Kt, Kn = KT[par], "KT%d" % par
                    Va, Vn = VA[par], "VA%d" % par
                    steps = []
                    for c in range(NQ):
                        def st(c=c):
                            cs = slice(c * 512, (c + 1) * 512)
                            op("pe", ["wkv2", "ckvnT%d" % c], [bn], lambda e: e.matmul(bank[0:64, :], lhsT=wkvv[:, h, 0:64], rhs=ckvnT[:, cs], start=True, stop=True))
                            copy_op("dve", [bn], [Kn], Kt[0:64, cs], bank[0:64, :])
                        steps.append(st)

                    def st_kr(st0=steps[0]):
                        op("dve", allA("krT"), [Kn], lambda e: e.tensor_copy(out=Kt[64:96, :], in_=krT[64:96, :]))
                        st0()
                    steps[0] = st_kr
                    for t8 in range(4):
                        def st(t8=t8):
                            def f(e):
                                for j in range(8):
                                    t = t8 * 8 + j
                                    i_ = e.matmul(bank[:, j * 64:(j + 1) * 64], lhsT=ckvnT[:, t * 128:(t + 1) * 128], rhs=wkvv[:, h, 64:128], start=True, stop=True)
                                return i_
                            op("pe", ["wkv2"] + allA("ckvnT"), [bn], f)
                            vo = 64 if par else 0
                            copy_op("dve", [bn], [Vn], Va[:, t8 * 8:(t8 + 1) * 8, vo:vo + 64], bank[:, :].rearrange("p (j c) -> p j c", c=64))
                        steps.append(st)
                    return steps

                def gen_q(h, qg, qi):
                    qs = slice(qg * 512, (qg + 1) * 512)
                    Qt, Qn = QT[qi % 2], "QT%d" % (qi % 2)
                    qa, qan = PS[6 + qi % 2], "PS%d" % (6 + qi % 2)

                    def f(e):
                        for k in range(2):
                            i_ = e.matmul(qa[:, :], lhsT=wq[:, k, h, :], rhs=cqnT[:, k, qs], start=(k == 0), stop=(k == 1))
                        return i_
                    op("pe", ["wq", "cqnT%d" % qg], [qan], f)
                    copy_op("dve", [qan], [Qn], Qt[0:64, :], qa[0:64, :])
                    op("dve", [qan, "cosT"], ["tq1"], lambda e: e.tensor_tensor(out=tq1[64:96, :], in0=qa[64:96, :], in1=rope_tab(cosT, qg), op=ALU.mult))
                    op("dve", [qan, "sinT"], ["tq2"], lambda e: e.tensor_tensor(out=tq2[64:96, :], in0=qa[96:128, :], in1=rope_tab(sinT, qg), op=ALU.mult))
                    op("dve", ["tq1", "tq2"], [Qn], lambda e: e.tensor_tensor(out=Qt[64:96, :], in0=tq1[64:96, :], in1=tq2[64:96, :], op=ALU.add))
                    if h == 0 and qg == 0:
                        dump("QT", [Qn], Qt[0:96, :])

                items = [(h, qg) for h in range(8) for qg in range(NQ)]
                LA = 2
                NPAIR = NT // 2
                for st in gen_kv_steps(0, PS[7], "PS7"):
                    st()
                gen_q(0, 0, 0)
                kv_steps = []
                stream = [(qi, p) for qi in range(len(items)) for p in range(NPAIR)]
                slots = {}

                def score_pair(n):
                    qi, p = stream[n]
                    h, qg = items[qi]
                    par = h % 2
                    Kt, Kn = KT[par], "KT%d" % par
                    Qt, Qn = QT[qi % 2], "QT%d" % (qi % 2)
                    j = next_slot()
                    slots[n] = j

                    def f(e):
                        for i in range(2):
                            kt = 2 * p + i
                            i_ = e.matmul(PS[2 * j + i][:, :], lhsT=Kt[0:96, kt * 128:(kt + 1) * 128], rhs=Qt[0:96, :], start=True, stop=True)
                        return i_
                    op("pe", [Kn, Qn], ["PS%d" % (2 * j), "PS%d" % (2 * j + 1)], f)
                for n in range(LA):
                    score_pair(n)
                for n, (qi, p) in enumerate(stream):
                    h, qg = items[qi]
                    par = h % 2
                    Va, Vn = VA[par], "VA%d" % par
                    qs = slice(qg * 512, (qg + 1) * 512)
                    pv, pvn = PS[6 + qi % 2], "PS%d" % (6 + qi % 2)
                    if n + LA < len(stream):
                        score_pair(n + LA)
                    j = slots.pop(n)
                    pt, ptn = PTp[n % 3], "PTp%d" % (n % 3)
                    op("act", ["PS%d" % (2 * j), "PS%d" % (2 * j + 1)], [ptn], lambda e: e.activation(out=pt[:], in_=PSP[j][:, :], func=AF.Exp, scale=scale))

                    def f(e):
                        for i in range(2):
                            kt = 2 * p + i
                            i_ = e.matmul(pv[:, :], lhsT=Va[:, kt, :], rhs=pt[:, i * 512:(i + 1) * 512], start=(kt == 0), stop=(kt == NT - 1))
                        return i_
                    op("pe", [ptn, Vn], [pvn], f)
                    release(j)
                    if p == 8 and qi + 1 < len(items):
                        gen_q(items[qi + 1][0], items[qi + 1][1], qi + 1)
                    if p == 2 and qg == 3 and h + 1 < 8:
                        kv_steps = gen_kv_steps(h + 1, PS[6 + (qi + 1) % 2], "PS%d" % (6 + (qi + 1) % 2))
                    if p >= 3 and p != 8 and kv_steps:
                        kv_steps.pop(0)()
                    if p == NPAIR - 1:
                        assert not kv_steps
                        ro = slice(0, 64) if par == 0 else slice(64, 128)
                        rs_ = slice(64, 128) if par == 0 else slice(0, 64)
                        op("dve", [pvn], ["rinv"], lambda e: e.reciprocal(out=rinv[ro, :], in_=pv[rs_, :]))
                        op("dve", [pvn, "rinv"], ["ymlaT%d" % qg], lambda e: e.tensor_tensor(out=ymlaT[ro, h // 2, qs], in0=pv[ro, :], in1=rinv[ro, :], op=ALU.mult))
                dump("ymlaT", allA("ymlaT"), ymlaT[:])
                kb.barrier()
        esA.close()

        if "S" in phases:
            G_ = dict(locals())
            G_["ssm_tables"] = ssm_tables_
            ssm_phase(kb, es0, G_)
        esT.close()

        if "C" in phases:
            phase_c(kb, es0, locals())

        kb.barrier()
    return nc


def ssm_tables_alloc(kb, esT):
    jm = kb.sb(esT, "jm", [128, 128], F32)
    bd = kb.sb(esT, "bd", [128, 128], F32)
    jm_b = kb.sb(esT, "jm_b", [128, 128], BF16)
    prb16 = kb.sb(esT, "prb16", [128, 2, 9, 32], BF16)
    pib16 = kb.sb(esT, "pib16", [128, 2, 9, 32], BF16)
    P1 = kb.sb(esT, "P1", [128, 2, 9, 32], F32)
    P2 = kb.sb(esT, "P2", [128, 2, 9, 32], F32)
    Q1 = kb.sb(esT, "Q1", [128, 2, 9, 32], F32)
    Q2 = kb.sb(esT, "Q2", [128, 2, 9, 32], F32)
    zre = kb.sb(esT, "zre", [128, 2, 32], F32)
    zim = kb.sb(esT, "zim", [128, 2, 32], F32)
    return dict(P1=P1, P2=P2, Q1=Q1, Q2=Q2, zre=zre, zim=zim, jm_b=jm_b, prb16=prb16, pib16=pib16, jm=jm, bd=bd)


def ssm_tables(kb, T, es2, G):
    nc = kb.nc
    op, dma = kb.op, kb.dma
    PS, ident_f, dump = G["PS"], G["ident_f"], G["dump"]
    cjm, cbd = G["cjm"], G["cbd"]
    lam_re, lam_im, log_dt = G["lam_re"], G["lam_im"], G["log_dt"]
    NP = 17
    P1, P2, Q1, Q2, zre, zim, jm_b, prb16, pib16, jm, bd = (T[k] for k in ("P1", "P2", "Q1", "Q2", "zre", "zim", "jm_b", "prb16", "pib16", "jm", "bd"))
    dma("sp", [], ["jm"], jm[:], cjm)
    dma("sp", [], ["bd"], bd[:], cbd)
    pr = kb.sb(es2, "pr", [128, 2, NP, 32], F32)
    pi = kb.sb(es2, "pi", [128, 2, NP, 32], F32)
    lt = kb.sb(es2, "lt", [32, 2, 2, 128], F32)
    for d in range(2):
        for ri, src in enumerate((lam_re, lam_im)):
            for hlf in range(2):
                dma("sp", [], ["lt"], lt[:, d, ri, hlf * 64:(hlf + 1) * 64], src[d])
    lr = kb.sb(es2, "lr", [128, 2, 32], F32)
    li = kb.sb(es2, "li", [128, 2, 32], F32)
    dtt = kb.sb(es2, "dtt", [128, 2, 32], F32)
    for d in range(2):
        for ri, dst in enumerate((lr, li)):
            op("pe", ["lt", "ident_f"], ["PS0"], lambda e: e.matmul(PS[0][:, 0:32], lhsT=lt[:, d, ri, :], rhs=ident_f[0:32, 0:32], start=True, stop=True))
            op("dve", ["PS0"], ["lrli"], lambda e: e.tensor_copy(out=dst[:, d, :], in_=PS[0][:, 0:32]))
    dma("sp", [], ["dtt"], dtt[:].rearrange("p d g -> p (d g)"), log_dt.rearrange("(o d) g -> o (d g)", o=1).partition_broadcast(128))
    tA = kb.sb(es2, "tA", [128, 2, 32], F32)
    tB = kb.sb(es2, "tB", [128, 2, 32], F32)
    tC = kb.sb(es2, "tC", [128, 2, 32], F32)
    tI = kb.sb(es2, "tI", [128, 2, 32], I32)
    dec = kb.sb(es2, "dec", [128, 2, 32], F32)
    X = ["sprm"]
    op("act", ["dtt"], X, lambda e: e.activation(out=dtt[:], in_=dtt[:], func=AF.Exp))
    op("dve", ["lrli"] + X, X, lambda e: e.tensor_tensor(out=tA[:], in0=lr[:], in1=dtt[:], op=ALU.mult))
    op("act", X, X, lambda e: e.activation(out=dec[:], in_=tA[:], func=AF.Exp))
    op("dve", ["lrli"] + X, X, lambda e: e.tensor_tensor(out=tA[:], in0=li[:], in1=dtt[:], op=ALU.mult))

    def sin_of(dst, shift):
        op("dve", X, X, lambda e: e.tensor_scalar(out=tB[:], in0=tA[:], scalar1=shift, scalar2=1.0 / TWO_PI, op0=ALU.add, op1=ALU.mult))
        op("dve", X, X, lambda e: e.tensor_copy(out=tI[:], in_=tB[:]))
        op("dve", X, X, lambda e: e.tensor_copy(out=tB[:], in_=tI[:]))
        op("dve", X, X, lambda e: e.tensor_scalar(out=tC[:], in0=tA[:], scalar1=shift, scalar2=None, op0=ALU.add))
        op("dve", X, X, lambda e: e.scalar_tensor_tensor(out=tC[:], in0=tB[:], scalar=-C1, in1=tC[:], op0=ALU.mult, op1=ALU.add))
        op("dve", X, X, lambda e: e.scalar_tensor_tensor(out=tC[:], in0=tB[:], scalar=-C2, in1=tC[:], op0=ALU.mult, op1=ALU.add))
        op("dve", X, X, lambda e: e.tensor_scalar(out=tC[:], in0=tC[:], scalar1=-math.pi, scalar2=math.pi, op0=ALU.max, op1=ALU.min))
        op("act", X, X, lambda e: e.activation(out=dst, in_=tC[:], func=AF.Sin))
    sn = kb.sb(es2, "sn", [128, 2, 32], F32)
    cs_ = kb.sb(es2, "cs_", [128, 2, 32], F32)
    sin_of(sn[:], 0.0)
    sin_of(cs_[:], math.pi / 2)
    op("dve", X, X, lambda e: e.memset(pr[:, :, 0, :], 1.0))
    op("dve", X, X, lambda e: e.memset(pi[:, :, 0, :], 0.0))
    op("dve", X, X, lambda e: e.tensor_tensor(out=pr[:, :, 1, :], in0=dec[:], in1=cs_[:], op=ALU.mult))
    op("dve", X, X, lambda e: e.tensor_tensor(out=pi[:, :, 1, :], in0=dec[:], in1=sn[:], op=ALU.mult))

    def cmul(io, ia, ib):
        op("dve", X, X, lambda e: e.tensor_tensor(out=tA[:], in0=pr[:, :, ia, :], in1=pr[:, :, ib, :], op=ALU.mult))
        op("dve", X, X, lambda e: e.tensor_tensor(out=tB[:], in0=pi[:, :, ia, :], in1=pi[:, :, ib, :], op=ALU.mult))
        op("dve", X, X, lambda e: e.tensor_tensor(out=pr[:, :, io, :], in0=tA[:], in1=tB[:], op=ALU.subtract))
        op("dve", X, X, lambda e: e.tensor_tensor(out=tA[:], in0=pr[:, :, ia, :], in1=pi[:, :, ib, :], op=ALU.mult))
        op("dve", X, X, lambda e: e.tensor_tensor(out=tB[:], in0=pi[:, :, ia, :], in1=pr[:, :, ib, :], op=ALU.mult))
        op("dve", X, X, lambda e: e.tensor_tensor(out=pi[:, :, io, :], in0=tA[:], in1=tB[:], op=ALU.add))
    for k in range(2, 9):
        cmul(k, k - 1, 1)
    for k in range(9, NP):
        cmul(k, k - 1, k - 1)
    den = kb.sb(es2, "den", [128, 2, 32], F32)
    fre = kb.sb(es2, "fre", [128, 2, 32], F32)
    op("dve", X, X, lambda e: e.tensor_tensor(out=tA[:], in0=lr[:], in1=lr[:], op=ALU.mult))
    op("dve", X, X, lambda e: e.tensor_tensor(out=tB[:], in0=li[:], in1=li[:], op=ALU.mult))
    op("dve", X, X, lambda e: e.tensor_tensor(out=den[:], in0=tA[:], in1=tB[:], op=ALU.add))
    op("dve", X, X, lambda e: e.reciprocal(out=den[:], in_=den[:]))
    op("dve", X, X, lambda e: e.tensor_scalar(out=fre[:], in0=pr[:, :, 1, :], scalar1=-1.0, scalar2=None, op0=ALU.add))
    op("dve", X, X, lambda e: e.tensor_tensor(out=tA[:], in0=fre[:], in1=lr[:], op=ALU.mult))
    op("dve", X, X, lambda e: e.tensor_tensor(out=tB[:], in0=pi[:, :, 1, :], in1=li[:], op=ALU.mult))
    op("dve", X, X, lambda e: e.tensor_tensor(out=tA[:], in0=tA[:], in1=tB[:], op=ALU.add))
    op("dve", X, X, lambda e: e.tensor_tensor(out=zre[:], in0=tA[:], in1=den[:], op=ALU.mult))
    op("dve", X, X, lambda e: e.tensor_tensor(out=tA[:], in0=pi[:, :, 1, :], in1=lr[:], op=ALU.mult))
    op("dve", X, X, lambda e: e.tensor_tensor(out=tB[:], in0=fre[:], in1=li[:], op=ALU.mult))
    op("dve", X, X, lambda e: e.tensor_tensor(out=tA[:], in0=tA[:], in1=tB[:], op=ALU.subtract))
    op("dve", X, X, lambda e: e.tensor_tensor(out=zim[:], in0=tA[:], in1=den[:], op=ALU.mult))
    lo, hi = slice(0, 64), slice(64, 128)
    op("dve", X, X, lambda e: e.tensor_copy(out=P1[lo], in_=pr[lo, :, 0:9, :]))
    op("dve", X, X, lambda e: e.tensor_copy(out=P1[hi], in_=pi[hi, :, 0:9, :]))
    op("dve", X, X, lambda e: e.tensor_scalar(out=P2[lo], in0=pi[lo, :, 0:9, :], scalar1=-1.0, scalar2=None, op0=ALU.mult))
    op("dve", X, X, lambda e: e.tensor_copy(out=P2[hi], in_=pr[hi, :, 0:9, :]))
    op("dve", X, X, lambda e: e.tensor_copy(out=Q1[lo], in_=pr[lo, :, 0:9, :]))
    op("dve", X, X, lambda e: e.tensor_scalar(out=Q1[hi], in0=pi[hi, :, 0:9, :], scalar1=-1.0, scalar2=None, op0=ALU.mult))
    op("dve", X, X, lambda e: e.tensor_scalar(out=Q2[lo], in0=pi[lo, :, 0:9, :], scalar1=-1.0, scalar2=None, op0=ALU.mult))
    op("dve", X, X, lambda e: e.tensor_scalar(out=Q2[hi], in0=pr[hi, :, 0:9, :], scalar1=-1.0, scalar2=None, op0=ALU.mult))
    op("dve", ["jm"], ["jm_b"], lambda e: e.tensor_copy(out=jm_b[:], in_=jm[:]))
    op("dve", X, ["prb16"], lambda e: e.tensor_copy(out=prb16[:], in_=pr[:, :, 8:17, :]))
    op("dve", X, ["pib16"], lambda e: e.tensor_copy(out=pib16[:], in_=pi[:, :, 8:17, :]))
    dump("pr", X, pr[:])
    dump("pi", X, pi[:])
    dump("zre", X, zre[:])
    pass

    return dict(P1=P1, P2=P2, Q1=Q1, Q2=Q2, zre=zre, zim=zim, jm_b=jm_b, prb16=prb16, pib16=pib16, jm=jm, bd=bd)


def ssm_phase(kb, es0, G):
    nc = kb.nc
    op, dma = kb.op, kb.dma
    PS, uT, ident_f, ident_b, ccols, dump = G["PS"], G["uT"], G["ident_f"], G["ident_b"], G["ccols"], G["dump"]
    cjm, cbd = G["cjm"], G["cbd"]
    lam_re, lam_im, log_dt, b_re, b_im, c_re, c_im = (G[k] for k in ("lam_re", "lam_im", "log_dt", "b_re", "b_im", "c_re", "c_im"))
    ssm_d, glu_w, glu_b = G["ssm_d"], G["glu_w"], G["glu_b"]
    copy_op, evac_eng = G["copy_op"], G["evac_eng"]
    NP = 17
    NCH = 512
    with ExitStack() as es:
        dcol = kb.sb(es, "dcol", [128, 4], F32)
        dma("sp", [], ["dcol"], dcol[:], ssm_d.rearrange("o (k p) -> p (o k)", p=128), allow_slow_non_contiguous=True)
        gbcol = kb.sb(es, "gbcol", [128, 4], F32)
        dma("sp", [], ["gbcol"], gbcol[:], glu_b.rearrange("o (k p) -> p (o k)", p=128), allow_slow_non_contiguous=True)
        yacc = kb.sb(es, "yacc", [128, L], F32)
        T = G["ssm_tables"]
        P1, P2, Q1, Q2, zre, zim, jm_b, prb16, pib16, jm, bd = (T[k] for k in ("P1", "P2", "Q1", "Q2", "zre", "zim", "jm_b", "prb16", "pib16", "jm", "bd"))
        bre = [kb.sb(es, "bre%d" % i, [128, 8, 16], F32) for i in range(2)]
        bim = [kb.sb(es, "bim%d" % i, [128, 8, 16], F32) for i in range(2)]
        ctl = [kb.sb(es, "ctl%d" % i, [128, 2, 128], F32) for i in range(2)]
        bbre = kb.sb(es, "bbre", [128, 8, 16], F32)
        bbim = kb.sb(es, "bbim", [128, 8, 16], F32)
        tb1 = kb.sb(es, "tb1", [128, 8, 16], F32)
        tb2 = kb.sb(es, "tb2", [128, 8, 16], F32)
        creD = [kb.sb(es, "creD%d" % i, [128, 8, 16], F32) for i in range(2)]
        cimD = [kb.sb(es, "cimD%d" % i, [128, 8, 16], F32) for i in range(2)]
        ATB = [kb.sb(es, "ATB%d" % i, [128, 8, 8, 16], BF16) for i in range(2)]
        tw1 = kb.sb(es, "tw1", [128, 8, 8, 16], F32)
        tw2 = kb.sb(es, "tw2", [128, 8, 8, 16], F32)
        GBm = [kb.sb(es, "GB%d" % i, [128, 8, 128], BF16) for i in range(2)]
        OT = kb.sb(es, "OT", [128, 8, 8, 128], BF16)
        cmT = [kb.sb(es, "cmT%d" % i, [128, 8, 16], BF16) for i in range(2)]
        Kb = kb.sb(es, "Kb", [128, 8, 128], BF16)
        ATs = [kb.sb(es, "AT%d" % i, [128, 8, 9, 128], BF16) for i in range(2)]
        att = kb.sb(es, "att", [128, 2, 9, 128], BF16)
        Sb = [[kb.sb(es, "S%d_%d" % (g, i), [128, NCH], BF16) for i in range(2)] for g in range(8)]
        op("pool", [], ["OT"], lambda e: e.memset(OT[:], 0.0))

        UTd = kb.sb(es, "UTd", [128, 8, NCH], BF16)
        OTd = RawAP(OT, 0, [[OT[:].ap[0][0], 128], [8 * 128 + 16, 8], [128, 8], [1, 16]])
        ident_bb = ident_b[:].unsqueeze(1).unsqueeze(1).to_broadcast([128, 2, 9, 128])
        jm_bb = jm_b[:].unsqueeze(1).unsqueeze(1).to_broadcast([128, 2, 9, 128])

        def gen_AT(idx):
            ct_, d_ = combos[idx]
            ATx = ATs[idx % 2]
            for g2 in range(4):
                gsl = slice(ct_ * 8 + g2 * 2, ct_ * 8 + g2 * 2 + 2)
                ATn = ["AT%d_%d" % (idx % 2, g) for g in range(g2 * 2, g2 * 2 + 2)]
                prb = prb16[:, d_, :, gsl].rearrange("p l g -> p g l").unsqueeze(3).to_broadcast([128, 2, 9, 128])
                pib = pib16[:, d_, :, gsl].rearrange("p l g -> p g l").unsqueeze(3).to_broadcast([128, 2, 9, 128])
                op("pool", ["jm_b", "pib16"], ["att"], lambda e: e.tensor_tensor(out=att[:], in0=jm_bb, in1=pib, op=ALU.mult))
                op("pool", ["ident_b", "prb16"], ATn, lambda e: e.tensor_tensor(out=ATx[:, g2 * 2:(g2 + 1) * 2], in0=ident_bb, in1=prb, op=ALU.mult))
                op("pool", ["att"] + ATn, ATn, lambda e: e.tensor_tensor(out=ATx[:, g2 * 2:(g2 + 1) * 2], in0=ATx[:, g2 * 2:(g2 + 1) * 2], in1=att[:], op=ALU.add))
        combos = [(ct, d) for ct in range(4) for d in range(2)]

        def load_params(idx):
            ct, d = combos[idx]
            i = idx % 2
            gs = slice(ct * 8, ct * 8 + 8)
            for hlf in range(2):
                hs = slice(hlf * 64, hlf * 64 + 64)
                dma("sp", [], ["bre%d" % i], bre[i][hs], b_re[d, gs].rearrange("g n q -> n g q"))
                dma("sp", [], ["bim%d" % i], bim[i][hs], b_im[d, gs].rearrange("g n q -> n g q"))
                dma("sp", [], ["ctl%d" % i], ctl[i][:, 0, hs], c_re[d, ct * 128:(ct + 1) * 128, :])
                dma("sp", [], ["ctl%d" % i], ctl[i][:, 1, hs], c_im[d, ct * 128:(ct + 1) * 128, :])
        def prep(idx):
            ct_, d_ = combos[idx]
            gs_ = slice(ct_ * 8, ct_ * 8 + 8)
            pb = idx % 2
            bre_, bim_, ctl_ = bre[pb], bim[pb], ctl[pb]
            bn, bin_, cn = "bre%d" % pb, "bim%d" % pb, "ctl%d" % pb
            creD_, cimD_, ATB_, cmT_ = creD[pb], cimD[pb], ATB[pb], cmT[pb]
            for ri, (dst, dn) in enumerate(((creD_, "creD%d" % pb), (cimD_, "cimD%d" % pb))):
                op("pe", [cn, "ident_f"], ["PS0"], lambda e: e.matmul(PS[0][:, 0:128], lhsT=ctl_[:, ri, :], rhs=ident_f[:], start=True, stop=True))
                op("dve", ["PS0"], [dn], lambda e: e.tensor_copy(out=dst[:].rearrange("p g q -> p (g q)"), in_=PS[0][:, 0:128]))
            zr_b = zre[:, d_, gs_].unsqueeze(2).to_broadcast([128, 8, 16])
            zi_b = zim[:, d_, gs_].unsqueeze(2).to_broadcast([128, 8, 16])
            op("dve", [bn], ["tb1"], lambda e: e.tensor_tensor(out=tb1[:], in0=bre_[:], in1=zr_b, op=ALU.mult))
            op("pool", [bin_], ["tb2"], lambda e: e.tensor_tensor(out=tb2[:], in0=bim_[:], in1=zi_b, op=ALU.mult))
            op("dve", ["tb1", "tb2"], ["bbre"], lambda e: e.tensor_tensor(out=bbre[:], in0=tb1[:], in1=tb2[:], op=ALU.subtract))
            op("dve", [bin_], ["tb1"], lambda e: e.tensor_tensor(out=tb1[:], in0=bim_[:], in1=zr_b, op=ALU.mult))
            op("pool", [bn], ["tb2"], lambda e: e.tensor_tensor(out=tb2[:], in0=bre_[:], in1=zi_b, op=ALU.mult))
            op("dve", ["tb1", "tb2"], ["bbim"], lambda e: e.tensor_tensor(out=bbim[:], in0=tb1[:], in1=tb2[:], op=ALU.add))
            sh = [128, 8, 8, 16]
            op("dve", ["bbre"], ["tw1"], lambda e: e.tensor_tensor(out=tw1[:], in0=P1[:, d_, 0:8, gs_].unsqueeze(3).to_broadcast(sh), in1=bbre[:].unsqueeze(1).to_broadcast(sh), op=ALU.mult))
            op("pool", ["bbim"], ["tw2"], lambda e: e.tensor_tensor(out=tw2[:], in0=P2[:, d_, 0:8, gs_].unsqueeze(3).to_broadcast(sh), in1=bbim[:].unsqueeze(1).to_broadcast(sh), op=ALU.mult))
            op("dve", ["tw1", "tw2"], ["ATB%d" % pb], lambda e: e.tensor_tensor(out=ATB_[:], in0=tw1[:], in1=tw2[:], op=ALU.add))
            op("dve", ["creD%d" % pb], ["cmT%d" % pb], lambda e: e.tensor_copy(out=cmT_[0:64], in_=creD_[0:64]))
            op("dve", ["cimD%d" % pb], ["cmT%d" % pb], lambda e: e.tensor_scalar(out=cmT_[64:128], in0=cimD_[64:128], scalar1=-1.0, scalar2=None, op0=ALU.mult))

        load_params(0)
        load_params(1)
        prep(0)
        gen_AT(0)
        for idx, (ct, d) in enumerate(combos):
            UT = uT[:, ct, :]
            UT3 = UTd[:]
            Un = ["UTd"]
            gs = slice(ct * 8, ct * 8 + 8)
            if d == 0:
                op("dve", ["uT%d_%d" % (ct, c) for c in range(NQ)], ["UTd"], lambda e: e.tensor_copy(out=UTd[:], in_=UT.rearrange("p (c j) -> p j c", j=8)))
            pb = idx % 2
            if idx + 2 < len(combos):
                load_params(idx + 2)
            creD_, cimD_, ATB_, cmT_ = creD[pb], cimD[pb], ATB[pb], cmT[pb]
            ATBn, cmTn, creDn, cimDn = "ATB%d" % pb, "cmT%d" % pb, "creD%d" % pb, "cimD%d" % pb
            for t4 in range(2):
                def f(e):
                    for t_ in range(4):
                        tau = t4 * 4 + t_
                        i_ = e.matmul(PS[0][:, t_ * 128:(t_ + 1) * 128], lhsT=ATB_[:, tau].rearrange("p g q -> p (g q)"), rhs=ident_b[:], start=True, stop=True)
                    return i_
                op("pe", [ATBn, "ident_b"], ["PS0"], f)
                ps3 = PS[0][:, :].rearrange("p (t c) -> p t c", c=128)
                op("dve", ["PS0", "ccols"], ["GB0"], lambda e: e.tensor_scalar(out=GBm[0][:, t4 * 4:(t4 + 1) * 4, :], in0=ps3, scalar1=ccols[:, 2:3], scalar2=None, op0=ALU.mult))
                op("dve", ["PS0", "ccols"], ["GB1"], lambda e: e.tensor_scalar(out=GBm[1][:, t4 * 4:(t4 + 1) * 4, :], in0=ps3, scalar1=ccols[:, 3:4], scalar2=None, op0=ALU.mult))

            def gen_out_weights():
                for t4 in range(2):
                    def f(e):
                        for t_ in range(4):
                            tau = t4 * 4 + t_
                            i_ = e.matmul(PS[1][:, t_ * 128:(t_ + 1) * 128], lhsT=ATB_[:, tau].rearrange("p g q -> p (g q)"), rhs=cmT_[:].rearrange("p g q -> p (g q)"), start=True, stop=True)
                        return i_
                    op("pe", [ATBn, cmTn], ["PS1"], f)
                    op("dve", ["PS1", "bd"], ["Kb"], lambda e: e.tensor_tensor(out=Kb[:, t4 * 4:(t4 + 1) * 4, :], in0=PS[1][:, :].rearrange("p (t c) -> p t c", c=128), in1=bd[:].unsqueeze(1).to_broadcast([128, 4, 128]), op=ALU.mult))
                shk = [128, 8, 8, 16]
                op("dve", [creDn], ["tw1"], lambda e: e.tensor_tensor(out=tw1[:], in0=Q1[:, d, 1:9, gs].unsqueeze(3).to_broadcast(shk), in1=creD_[:].unsqueeze(1).to_broadcast(shk), op=ALU.mult))
                op("pool", [cimDn], ["tw2"], lambda e: e.tensor_tensor(out=tw2[:], in0=Q2[:, d, 1:9, gs].unsqueeze(3).to_broadcast(shk), in1=cimD_[:].unsqueeze(1).to_broadcast(shk), op=ALU.mult))
                op("dve", ["tw1", "tw2"], ["OT"], lambda e: e.tensor_tensor(out=OTd, in0=tw1[:].rearrange("p k g q -> p g k q"), in1=tw2[:].rearrange("p k g q -> p g k q"), op=ALU.add))
            Sn = lambda g, i: "S%d_%d" % (g, i)
            for par in range(2):
                def f(e):
                    for j in range(8):
                        tau = 7 - j if d == 0 else j
                        for pair in range(4):
                            rows = slice(32 * pair, 32 * pair + 32)
                            i_ = e.matmul(PS[2 + pair][:, :], lhsT=GBm[par][rows, tau, :], rhs=UT3[rows, j, :], start=(j == 0), stop=(j == 7), tile_position=(32 * pair, 0))
                    return i_
                op("pe", ["GB%d" % par] + Un, ["PS2", "PS3", "PS4", "PS5"], f)
                for pair in range(4):
                    g = 2 * pair + par
                    copy_op(evac_eng(), ["PS%d" % (2 + pair)], [Sn(g, 0)], Sb[g][0][:], PS[2 + pair][:, :])
            gen_out_weights()
            if idx + 1 < len(combos):
                gen_AT(idx + 1)
            AT = ATs[idx % 2]
            hsrr = 0
            for lev in range(9):
                s_ = 1 << lev
                cur = lev % 2
                for g in range(8):
                    src, dst = Sb[g][cur], Sb[g][1 - cur]
                    hp, hn_ = PS[2 + hsrr % 4], "PS%d" % (2 + hsrr % 4)
                    hsrr += 1

                    def f(e):
                        e.matmul(hp[:, :], lhsT=ident_b[:], rhs=src[:], start=True, stop=False)
                        if d == 0:
                            return e.matmul(hp[:, s_:NCH], lhsT=AT[:, g, lev, :], rhs=src[:, 0:NCH - s_], start=False, stop=True)
                        return e.matmul(hp[:, 0:NCH - s_], lhsT=AT[:, g, lev, :], rhs=src[:, s_:NCH], start=False, stop=True)
                    op("pe", ["AT%d_%d" % (idx % 2, g), Sn(g, cur), "ident_b"], [hn_], f)
                    copy_op(evac_eng(), [hn_], [Sn(g, 1 - cur)], dst[:], hp[:, :])
            if ct == 0 and d == 0:
                dump("S0", ["S0_1"], Sb[0][1][:])
            if idx + 1 < len(combos):
                prep(idx + 1)
            yacc3 = yacc[:].rearrange("p (c j) -> p j c", j=8)
            for j in range(8):
                yp, yn = PS[j % 2], "PS%d" % (j % 2)
                kk = j + 1 if d == 0 else 8 - j
                jps = list(range(0, j + 1)) if d == 0 else list(range(j, 8))

                def f(e):
                    for n_, jp in enumerate(jps):
                        e.matmul(yp[:, :], lhsT=Kb[:, abs(j - jp), :], rhs=UT3[:, jp, :], start=(n_ == 0), stop=False)
                    for g in range(8):
                        if d == 0:
                            i_ = e.matmul(yp[:, 1:NCH], lhsT=OT[:, g, kk - 1, :], rhs=Sb[g][1][:, 0:NCH - 1], start=False, stop=(g == 7))
                        else:
                            i_ = e.matmul(yp[:, 0:NCH - 1], lhsT=OT[:, g, kk - 1, :], rhs=Sb[g][1][:, 1:NCH], start=False, stop=(g == 7))
                    return i_
                op("pe", ["Kb", "OT"] + Un + ["S%d_1" % g for g in range(8)], [yn], f)
                if d == 0:
                    op("dve", [yn, "dcol"] + Un, ["yacc"], lambda e: e.scalar_tensor_tensor(out=yacc3[:, j, :], in0=UT3[:, j, :], scalar=dcol[:, ct:ct + 1], in1=yp[:, :], op0=ALU.mult, op1=ALU.add))
                else:
                    op("dve", [yn, "yacc"], ["yacc"], lambda e: e.tensor_tensor(out=yacc3[:, j, :], in0=yacc3[:, j, :], in1=yp[:, :], op=ALU.add))
            if d == 0:
                continue
            if ct == 0:
                dump("yacc", ["yacc"], yacc[:])
            for c in range(NQ):
                cs = slice(c * 512, (c + 1) * 512)
                op("act", ["yacc"], ["uT%d_%d" % (ct, c)], lambda e: e.activation(out=uT[:, ct, cs], in_=yacc[:, cs], func=AF.Gelu))
        kb.barrier()
        glt = ATs[0][:, 0:2].rearrange("p g l c -> p (g l c)")[:, 0:2048].rearrange("p (m t) -> p m t", m=4)
        wglu = ATs[0][:, 2:4].rearrange("p g l c -> p (g l c)")[:, 0:2048].rearrange("p (k c) -> p k c", k=4)
        dma("pool", [], ["wglu"], wglu, glu_w.rearrange("(k p) c -> p k c", p=128))
        sgbs = [ATs[1][:, i:i + 1].rearrange("p g l c -> p (g l c)")[:, 0:1024].bitcast(F32) for i in range(2)]
        for c in range(NQ):
            cs = slice(c * 512, (c + 1) * 512)
            Uc = ["uT%d_%d" % (m, c) for m in range(4)]
            for m in range(4):
                gp, gn = PS[m % 4], "PS%d" % (m % 4)
                sgb, sgn = sgbs[m % 2], "sg%d" % (m % 2)

                def f(e):
                    for k in range(4):
                        i_ = e.matmul(gp[:, :], lhsT=wglu[:, k, m * 128:(m + 1) * 128], rhs=uT[:, k, cs], start=(k == 0), stop=(k == 3))
                    return i_
                op("pe", ["wglu"] + Uc, [gn], f)
                op("act", [gn, "gbcol"], [sgn], lambda e: e.activation(out=sgb, in_=gp[:, :], func=AF.Sigmoid, bias=gbcol[:, m:m + 1], scale=1.0))
                op("dve", [sgn] + Uc, ["glt%d" % m], lambda e: e.tensor_tensor(out=glt[:, m, :], in0=uT[:, m, cs], in1=sgb, op=ALU.mult))
            for m in range(4):
                op("pool", ["glt%d" % m], [Uc[m]], lambda e: e.tensor_copy(out=uT[:, m, cs], in_=glt[:, m, :]))
        dump("yssmT", ["uT%d_%d" % (m, c) for m in range(4) for c in range(NQ)], uT[:])
        kb.barrier()


def phase_c(kb, es0, G):
    nc = kb.nc
    op, dma = kb.op, kb.dma
    PS, PSB, uT, ymlaT, kmemT, vmem, ones_b, ident_b = (G[k] for k in ("PS", "PSB", "uT", "ymlaT", "kmemT", "vmem", "ones_b", "ident_b"))
    w_br, w_out, b_gate, post_norm, x, out, wsc, wbsc, wosc = (G[k] for k in ("w_br", "w_out", "b_gate", "post_norm", "x", "out", "wsc", "wbsc", "wosc"))
    hT_norm, hT_tr, dump, copy_op, evac_eng = G["hT_norm"], G["hT_tr"], G["dump"], G["copy_op"], G["evac_eng"]
    with ExitStack() as es:
        wbr = kb.sb(es, "wbr", [128, 3, 4, D], BF16)
        wo = kb.sb(es, "wo", [128, 8, D], BF16)
        bg = kb.sb(es, "bg", [128, 24], F32)
        gpost = kb.sb(es, "gpost", [128, D], F32)
        xts = [kb.sb(es, "c_xt%d" % i, [128, D], F32) for i in range(2)]
        hbs = [kb.sb(es, "c_hb%d" % i, [128, D], BF16) for i in range(3)]
        junkc = kb.sb(es, "junkc", [128, D], BF16)
        ssx = kb.sb(es, "c_ssx", [128, 3, 4], F32)
        hTs = [kb.sb(es, "c_hT%d" % i, [128, 8, 512], BF16) for i in range(2)]
        wz = [kb.sb(es, "wz%d" % i, [128, 2, 8, 128], BF16) for i in range(2)]
        wg = [kb.sb(es, "wg%d" % i, [128, 8, 128], BF16) for i in range(3)]
        actT = kb.sb(es, "actT", [128, 3, 4, 512], BF16)
        qmT = kb.sb(es, "qmT", [128, 4, 512], BF16)
        ymemT = kb.sb(es, "ymemT", [128, 4, 512], BF16)
        pmTs = [kb.sb(es, "pmT%d" % i, [128, 2, 512], BF16) for i in range(2)]
        szs = [kb.sb(es, "sz%d" % i, [128, 512], BF16) for i in range(2)]
        prr = [0]

        def pbank():
            i = (0, 1, 6)[prr[0] % 3]
            prr[0] += 1
            return PS[i], "PS%d" % i
        gt = [kb.sb(es, "gt%d" % i, [128, 512], BF16) for i in range(2)]
        mt = kb.sb(es, "mt", [128, 512], F32)
        mt2 = kb.sb(es, "mt2", [128, 512], F32)
        mergedT = kb.sb(es, "mergedT", [128, 8, 512], BF16)
        xr = kb.sb(es, "xr0", [128, D], F32)
        ot = kb.sb(es, "ot0", [128, D], F32)
        sso = kb.sb(es, "sso", [128, 8], F32)
        wzc = [0]
        wgc = [0]
        mscale = 128.0 ** -0.5

        class Prefetch:
            def __init__(self, seq, bufs, names, loader):
                self.seq, self.bufs, self.names, self.loader = seq, bufs, names, loader
                self.issued = 0
                self.used = 0
                for _ in range(len(bufs) - 1):
                    self.issue()

            def issue(self):
                if self.issued < len(self.seq):
                    i = self.issued % len(self.bufs)
                    self.loader(self.seq[self.issued], self.bufs[i], self.names[i])
                    self.issued += 1

            def next(self, expect):
                assert self.seq[self.used] == expect, (self.seq[self.used], expect)
                i = self.used % len(self.bufs)
                self.used += 1
                self.issue()
                return self.bufs[i], self.names[i]

        wz_seq = [mi for _ in range(NQ) for mi in (0, 2, 4, 6, 8, 10, 12, 14)]
        wg_seq = [(m, b) for _ in range(NQ) for m in range(8) for b in range(3)]
        pf_wz = Prefetch(wz_seq, wz, ["wz0", "wz1"], lambda mi, t, n: dma("sp", ["wsc"], [n], t[:], wsc[mi:mi + 2].rearrange("m p k c -> p m k c")))
        pf_wg = Prefetch(wg_seq, wg, ["wg0", "wg1", "wg2"], lambda mb, t, n: dma("sp", ["wsc"], [n], t[:], wsc[16 + 3 * mb[0] + mb[1]]))

        def load_wz(mi):
            return pf_wz.next(mi)

        def load_wg(m, b):
            return pf_wg.next((m, b))

        hT_norm(0, 0, xts, hbs, ssx, junkc, "junkc")
        hT_norm(0, 1, xts, hbs, ssx, junkc, "junkc")
        dma("sp", [], ["bg"], bg[:], b_gate.rearrange("o (k p) -> p (o k)", p=128), allow_slow_non_contiguous=True)
        dma("sp", [], ["gpost"], gpost[:], post_norm.partition_broadcast(128))
        for b in range(3):
            dma("sp", ["wbsc"], ["wbr"], wbr[:, b], wbsc[b])
        dma("sp", ["wosc"], ["wo"], wo[:], wosc)
        for i in range(4):
            hT_tr(0, i, hTs[0], "hTc0", hbs)
            if i + 2 < 4:
                hT_norm(0, i + 2, xts, hbs, ssx, junkc, "junkc")
        for c in range(NQ):
            cs = slice(c * 512, (c + 1) * 512)
            hT, hn = hTs[c % 2], "hTc%d" % (c % 2)

            def projT(p_, pn, wt, wn, mloc):
                def f(e):
                    for k in range(8):
                        i_ = e.matmul(p_[:, :], lhsT=wt[:, mloc, k, :], rhs=hT[:, k, :], start=(k == 0), stop=(k == 7))
                    return i_
                op("pe", [wn, hn], [pn], f)
            for b, (mi0, ysrc, ynm) in enumerate(((0, uT, lambda m: ["uT%d_%d" % (m, c)]), (4, ymlaT, lambda m: ["ymlaT%d" % c]))):
                for m in range(4):
                    if m % 2 == 0:
                        wt, wn = load_wz(mi0 + m)
                    p_, pn = pbank()
                    sz, szn = szs[m % 2], "sz%d" % (m % 2)
                    projT(p_, pn, wt, wn, m % 2)
                    op("act", [pn], [szn], lambda e: e.activation(out=sz[:], in_=p_[:, :], func=AF.Silu))
                    op("dve", [szn] + ynm(m), ["actT%d" % b], lambda e: e.tensor_tensor(out=actT[:, b, m, :], in0=sz[:], in1=ysrc[:, m, cs], op=ALU.mult))
            for m in range(4):
                if m % 2 == 0:
                    wt, wn = load_wz(8 + m)
                p_, pn = pbank()
                projT(p_, pn, wt, wn, m % 2)
                copy_op("dve", [pn], ["qmT%d" % m], qmT[:, m, :], p_[:, :])
            def mem_scores(h):
                pm, pmn = pmTs[h % 2], "pmT%d" % (h % 2)
                for mt_ in range(2):
                    bi = 2 + mt_ + 2 * (h % 2)
                    p_, pn = PS[bi], "PS%d" % bi
                    op("pe", ["kmemT", "qmT%d" % h], [pn], lambda e: e.matmul(p_[:, :], lhsT=kmemT[:, h, mt_ * 128:(mt_ + 1) * 128], rhs=qmT[:, h, :], start=True, stop=True))
                    op("act", [pn], [pmn], lambda e: e.activation(out=pm[:, mt_, :], in_=p_[:, :], func=AF.Exp, scale=mscale))
            mem_scores(0)
            for h in range(4):
                if h + 1 < 4:
                    mem_scores(h + 1)
                pm, pmn = pmTs[h % 2], "pmT%d" % (h % 2)
                pva, pvn = PS[6], "PS6"
                pra, prn = (PS[0], "PS0") if h % 2 == 0 else (PS[1], "PS1")

                def f(e):
                    for mt_ in range(2):
                        i_ = e.matmul(pva[:, :], lhsT=vmem[:, mt_, h * 128:(h + 1) * 128], rhs=pm[:, mt_, :], start=(mt_ == 0), stop=(mt_ == 1))
                    return i_
                op("pe", ["vmem", pmn], [pvn], f)

                def f(e):
                    for mt_ in range(2):
                        i_ = e.matmul(pra[:, :], lhsT=ones_b[:], rhs=pm[:, mt_, :], start=(mt_ == 0), stop=(mt_ == 1))
                    return i_
                op("pe", ["ones_b", pmn], [prn], f)
                op("act", [prn], ["mt"], lambda e: e.activation(out=mt[:], in_=pra[:, :], func=AF.Ln))
                op("act", ["mt"], ["mt"], lambda e: e.activation(out=mt[:], in_=mt[:], func=AF.Exp, scale=-1.0))
                op("dve", [pvn, "mt"], ["ymemT%d" % h], lambda e: e.tensor_tensor(out=ymemT[:, h, :], in0=pva[:, :], in1=mt[:], op=ALU.mult))
            if c == 0:
                dump("ymemT", ["ymemT%d" % h for h in range(4)], ymemT[:])
            for m in range(4):
                if m % 2 == 0:
                    wt, wn = load_wz(12 + m)
                p_, pn = pbank()
                sz, szn = szs[m % 2], "sz%d" % (m % 2)
                projT(p_, pn, wt, wn, m % 2)
                op("act", [pn], [szn], lambda e: e.activation(out=sz[:], in_=p_[:, :], func=AF.Silu))
                op("dve", [szn, "ymemT%d" % m], ["actT2"], lambda e: e.tensor_tensor(out=actT[:, 2, m, :], in0=sz[:], in1=ymemT[:, m, :], op=ALU.mult))
            it = 0
            for m in range(8):
                if m == 4 and c + 1 < NQ:
                    hT_norm(c + 1, 0, xts, hbs, ssx, junkc, "junkc")
                if m == 6 and c + 1 < NQ:
                    hT_norm(c + 1, 1, xts, hbs, ssx, junkc, "junkc")
                for b in range(3):
                    wgt, wgn = load_wg(m, b)
                    gp, gpn = PS[2 + it % 2], "PS%d" % (2 + it % 2)
                    bp, bpn = PS[4 + it % 2], "PS%d" % (4 + it % 2)
                    gtt, gtn = gt[it % 2], "gt%d" % (it % 2)
                    it += 1

                    def f(e):
                        for k in range(8):
                            i_ = e.matmul(gp[:, :], lhsT=wgt[:, k, :], rhs=hT[:, k, :], start=(k == 0), stop=(k == 7))
                        return i_
                    op("pe", [wgn, hn], [gpn], f)
                    op("act", [gpn, "bg"], [gtn], lambda e: e.activation(out=gtt[:], in_=gp[:, :], func=AF.Sigmoid, bias=bg[:, b * 8 + m:b * 8 + m + 1], scale=1.0))

                    def f(e):
                        for k in range(4):
                            i_ = e.matmul(bp[:, :], lhsT=wbr[:, b, k, m * 128:(m + 1) * 128], rhs=actT[:, b, k, :], start=(k == 0), stop=(k == 3))
                        return i_
                    op("pe", ["wbr", "actT%d" % b], [bpn], f)
                    if b == 0:
                        op("dve", [gtn, bpn], ["mt"], lambda e: e.tensor_tensor(out=mt[:], in0=gtt[:], in1=bp[:, :], op=ALU.mult))
                    else:
                        op("dve", [gtn, bpn], ["mt2"], lambda e: e.tensor_tensor(out=mt2[:], in0=gtt[:], in1=bp[:, :], op=ALU.mult))
                        if b == 1:
                            op("pool", ["mt", "mt2"], ["mt"], lambda e: e.tensor_tensor(out=mt[:], in0=mt[:], in1=mt2[:], op=ALU.add))
                        else:
                            op("pool", ["mt", "mt2"], ["mergedT"], lambda e: e.tensor_tensor(out=mergedT[:, m, :], in0=mt[:], in1=mt2[:], op=ALU.add))
            if c == 0:
                dump("mergedT", ["mergedT"], mergedT[:])
            for i in range(4):
                ti = c * 4 + i
                if c + 1 < NQ:
                    hT_tr(c + 1, i, hTs[(c + 1) % 2], "hTc%d" % ((c + 1) % 2), hbs)
                if c + 1 < NQ and i + 2 < 4:
                    hT_norm(c + 1, i + 2, xts, hbs, ssx, junkc, "junkc")
                dma("sp", [], ["xr0"], xr[:], x[ti * 128:(ti + 1) * 128, :])
                banks = (0, 1) if i % 2 == 0 else (6, 3)
                for half in range(2):
                    p_, pn = PS[banks[half]], "PS%d" % banks[half]

                    def f(e):
                        for k in range(8):
                            i_ = e.matmul(p_[:, :], lhsT=mergedT[:, k, i * 128:(i + 1) * 128], rhs=wo[:, k, half * 512:(half + 1) * 512], start=(k == 0), stop=(k == 7))
                        return i_
                    op("pe", ["mergedT", "wo"], [pn], f)
                    sl = (i % 2) * 4 + half
                    op("act", [pn], ["junkc", "sso%d" % (i % 2)], lambda e: e.activation(out=junkc[:, half * 512:(half + 1) * 512], in_=p_[:, :], func=AF.Square, accum_out=sso[:, sl:sl + 1]))
                so = (i % 2) * 4
                sn_ = "sso%d" % (i % 2)
                op("dve", [sn_], [sn_], lambda e: e.tensor_tensor(out=sso[:, so + 2:so + 3], in0=sso[:, so:so + 1], in1=sso[:, so + 1:so + 2], op=ALU.add))
                op("dve", [sn_], [sn_], lambda e: e.tensor_scalar(out=sso[:, so + 2:so + 3], in0=sso[:, so + 2:so + 3], scalar1=1.0 / D, scalar2=EPS, op0=ALU.mult, op1=ALU.add))
                op("act", [sn_], [sn_], lambda e: e.activation(out=sso[:, so + 2:so + 3], in_=sso[:, so + 2:so + 3], func=AF.Sqrt))
                op("dve", [sn_], [sn_], lambda e: e.reciprocal(out=sso[:, so + 3:so + 4], in_=sso[:, so + 2:so + 3]))
                for half in range(2):
                    hs = slice(half * 512, (half + 1) * 512)
                    op("dve", ["PS%d" % banks[half], sn_, "gpost"], ["ot0"], lambda e: e.scalar_tensor_tensor(out=ot[:, hs], in0=PS[banks[half]][:, :], scalar=sso[:, so + 3:so + 4], in1=gpost[:, hs], op0=ALU.mult, op1=ALU.mult))
                op("pool", ["ot0", "xr0"], ["ot0"], lambda e: e.tensor_tensor(out=ot[:], in0=ot[:], in1=xr[:], op=ALU.add))
                dma("pool", ["ot0"], ["outdram"], out[ti * 128:(ti + 1) * 128, :], ot[:])
        kb.barrier()


def host_consts():
    ident = np.eye(128, dtype=np.float32)
    jm = np.zeros((128, 128), np.float32)
    r = np.arange(64)
    jm[r, 64 + r] = 1.0
    jm[64 + r, r] = -1.0
    bd = np.kron(np.eye(8, dtype=np.float32), np.ones((16, 16), np.float32))
    col = np.zeros((128, 8), np.float32)
    p = np.arange(128)
    col[:, 0] = (10000.0 ** (-(2.0 * (p % 16)) / 32.0)).astype(np.float32)
    col[:, 1] = np.where((p % 32) < 16, -1.0, 1.0)
    col[:, 2] = ((p // 16) % 2 == 0)
    col[:, 3] = ((p // 16) % 2 == 1)
    return ident, jm, bd, col


def make_in_maps(inputs, cores):
    ident, jm, bd, col = host_consts()
    f = lambda a: np.ascontiguousarray(np.asarray(a, dtype=np.float32))
    shared = {
        "pre_norm": f(inputs["pre_norm"][0]).reshape(1, D), "w_in": f(inputs["w_in"][0]),
        "b_gate": f(inputs["b_gate"][0]).reshape(1, 3072),
        "lam_re": f(inputs["ssm_lambda_re"][0]), "lam_im": f(inputs["ssm_lambda_im"][0]), "log_dt": f(inputs["ssm_log_dt"][0]),
        "b_re": f(inputs["ssm_b_re"][0]), "b_im": f(inputs["ssm_b_im"][0]),
        "c_re": f(inputs["ssm_c_re"][0]).reshape(2, 512, 64), "c_im": f(inputs["ssm_c_im"][0]).reshape(2, 512, 64),
        "ssm_d": f(inputs["ssm_d"][0]).reshape(1, 512), "glu_w": f(inputs["ssm_glu_w"][0]), "glu_b": f(inputs["ssm_glu_b"][0]).reshape(1, 512),
        "q_norm": f(inputs["mla_q_norm"][0]).reshape(1, 256), "w_q_up": f(inputs["mla_w_q_up"][0]),
        "kv_norm": f(inputs["mla_kv_norm"][0]).reshape(1, 128), "w_kv_up": f(inputs["mla_w_kv_up"][0]),
        "mem_norm": f(inputs["mem_norm"][0]).reshape(1, D), "mem_w_kv": f(inputs["mem_w_kv"][0]),
        "w_br0": f(inputs["w_branch_ssm"][0]), "w_br1": f(inputs["w_branch_mla"][0]), "w_br2": f(inputs["w_branch_mem"][0]),
        "w_out": f(inputs["w_out"][0]), "post_norm": f(inputs["post_norm"][0]).reshape(1, D),
        "cident": ident, "cjm": jm, "cbd": bd, "ccol": col,
    }
    maps = []
    for b in cores:
        m = dict(shared)
        m["x"] = f(inputs["x"][b])
        m["mem"] = f(inputs["mem"][b])
        m["pos"] = np.ascontiguousarray(np.asarray(inputs["positions"][b], dtype=np.int32)).reshape(1, L)
        maps.append(m)
    return maps


def kernel(**inputs):
    nc = build()
    maps = make_in_maps(inputs, list(range(8)))
    res = run_bass_kernel_spmd(nc, maps, core_ids=list(range(8)))
    return np.stack([np.asarray(r["out"], dtype=np.float32) for r in res.results], axis=0)
```

```python
import math
import numpy as np
from contextlib import ExitStack
import concourse.bass as bass
import concourse.mybir as mybir
from concourse.bass_utils import run_bass_kernel_spmd
from concourse.ap import AP as RawAP

F32 = mybir.dt.float32
BF16 = mybir.dt.bfloat16
I32 = mybir.dt.int32
AF = mybir.ActivationFunctionType
ALU = mybir.AluOpType

L = 4096
D = 1024
NT = L // 128
NQ = L // 512
EPS = 1e-6
TWO_PI = 2.0 * math.pi
C1 = 6.28125
C2 = TWO_PI - C1
O_U, O_ZS, O_CQ, O_CKV, O_KR, O_ZM, O_QM, O_ZME, O_G = 0, 512, 1024, 1280, 1408, 1440, 1952, 2464, 2976


class Tok:
    __slots__ = ("w", "r")

    def __init__(self):
        self.w = None
        self.r = {}


class KB:
    def __init__(self, nc):
        self.nc = nc
        self.es = ExitStack()
        self.E = {"pe": nc.tensor, "act": nc.scalar, "dve": nc.vector, "pool": nc.gpsimd, "sp": nc.sync}
        self.sem = {}
        self.cnt = {}
        self.waited = {e: {} for e in self.E}
        for e in self.E:
            self.sem[e] = self.es.enter_context(nc.semaphore("sem_" + e))
            self.cnt[e] = 0
        self.dsem = {}
        for q, n in (("sp", 24), ("pool", 8)):
            self.dsem[q] = [[self.es.enter_context(nc.semaphore("d%s%d" % (q, i))), 0] for i in range(n)]
        self.dnext = {"sp": 0, "pool": 0}
        self.toks = {}
        self.same_engine_sync = True

    def tok(self, name):
        t = self.toks.get(name)
        if t is None:
            t = Tok()
            self.toks[name] = t
        return t

    def T(self, names):
        if isinstance(names, str):
            names = [names]
        return [self.tok(n) if isinstance(n, str) else n for n in names]

    def wait(self, eng, ev):
        sem, val = ev
        if sem is self.sem[eng]:
            if eng in ("pe", "sp") or not self.same_engine_sync:
                return
        w = self.waited[eng]
        if w.get(sem.num, 0) >= val:
            return
        self.E[eng].wait_ge(sem, val)
        w[sem.num] = val

    def deps(self, eng, R, W):
        for t in R:
            if t.w is not None:
                self.wait(eng, t.w)
        own = self.sem[eng] if eng in ("act", "dve") else None
        for t in W:
            for ev in t.r.values():
                if ev[0] is not own:
                    self.wait(eng, ev)
            if t.w is not None and t.w[0] is not own:
                self.wait(eng, t.w)

    def mark(self, R, W, ev):
        for t in R:
            t.r[ev[0].num] = ev
        for t in W:
            t.w = ev
            t.r = {}

    def op(self, eng, R, W, fn):
        R = self.T(R)
        W = self.T(W)
        self.deps(eng, R, W)
        inst = fn(self.E[eng])
        self.cnt[eng] += 1
        inst.then_inc(self.sem[eng], 1)
        self.mark(R, W, (self.sem[eng], self.cnt[eng]))

    def dma(self, q, R, W, out, in_, **kw):
        R = self.T(R)
        W = self.T(W)
        self.deps(q, R, W)
        pool = self.dsem[q]
        i = self.dnext[q]
        self.dnext[q] = (i + 1) % len(pool)
        sem, c = pool[i]
        if c > 0:
            self.wait(q, (sem, c))
        self.E[q].dma_start(out=out, in_=in_, **kw).then_inc(sem, 16)
        pool[i][1] = c + 16
        self.mark(R, W, (sem, c + 16))

    def barrier(self):
        evs = [(self.sem[e], self.cnt[e]) for e in self.E if self.cnt[e] > 0]
        for q in self.dsem:
            for sem, c in self.dsem[q]:
                if c > 0:
                    evs.append((sem, c))
        for e in self.E:
            for ev in evs:
                self.wait(e, ev)

    def sb(self, es, name, shape, dt):
        return es.enter_context(self.nc.sbuf_tensor(name, list(shape), dt))

    def ps(self, es, name, shape, dt):
        return es.enter_context(self.nc.psum_tensor(name, list(shape), dt))


def build(debug=None, phases="0AMSC"):
    nc = bass.Bass("TRN2", target_bir_lowering=False)
    kb = KB(nc)

    def din(name, shape, dt=F32):
        return nc.dram_tensor(name, list(shape), dt, kind="ExternalInput").ap()

    x = din("x", [L, D])
    mem = din("mem", [256, D])
    pos = din("pos", [1, L], I32)
    pre_norm = din("pre_norm", [1, D])
    w_in = din("w_in", [D, 6048])
    b_gate = din("b_gate", [1, 3072])
    lam_re = din("lam_re", [2, 32, 64])
    lam_im = din("lam_im", [2, 32, 64])
    log_dt = din("log_dt", [2, 32])
    b_re = din("b_re", [2, 32, 64, 16])
    b_im = din("b_im", [2, 32, 64, 16])
    c_re = din("c_re", [2, 512, 64])
    c_im = din("c_im", [2, 512, 64])
    ssm_d = din("ssm_d", [1, 512])
    glu_w = din("glu_w", [512, 512])
    glu_b = din("glu_b", [1, 512])
    q_norm = din("q_norm", [1, 256])
    w_q_up = din("w_q_up", [256, 768])
    kv_norm = din("kv_norm", [1, 128])
    w_kv_up = din("w_kv_up", [128, 1024])
    mem_norm = din("mem_norm", [1, D])
    mem_w_kv = din("mem_w_kv", [D, 1024])
    w_br = [din("w_br%d" % i, [512, D]) for i in range(3)]
    w_out = din("w_out", [D, D])
    post_norm = din("post_norm", [1, D])
    cident = din("cident", [128, 128])
    cjm = din("cjm", [128, 128])
    cbd = din("cbd", [128, 128])
    ccol = din("ccol", [128, 8])
    out = nc.dram_tensor("out", [L, D], F32, kind="ExternalOutput").ap()
    wsc = nc.dram_tensor("wsc", [40, 128, 8, 128], BF16, kind="Internal").ap()
    wbsc = nc.dram_tensor("wbsc", [3, 128, 4, D], BF16, kind="Internal").ap()
    wosc = nc.dram_tensor("wosc", [128, 8, D], BF16, kind="Internal").ap()
    dbg = {}
    if debug:
        for name, shape in debug.items():
            dbg[name] = nc.dram_tensor("dbg_" + name, list(shape), F32, kind="ExternalOutput").ap()

    op, dma = kb.op, kb.dma
    es0 = kb.es
    w_in_v = w_in.rearrange("(k p) c -> p k c", p=128)

    def col_view(v, n):
        return v.rearrange("o (k p) -> p (o k)", p=128)

    with es0:
        ident_f = kb.sb(es0, "ident_f", [128, 128], F32)
        ident_b = kb.sb(es0, "ident_b", [128, 128], BF16)
        ones_b = kb.sb(es0, "ones_b", [128, 128], BF16)
        ccols = kb.sb(es0, "ccols", [128, 8], F32)
        gpre = kb.sb(es0, "gpre", [128, 8], F32)
        uT = kb.sb(es0, "uT", [128, 4, L], BF16)
        ymlaT = kb.sb(es0, "ymlaT", [128, 4, L], BF16)
        kmemT = kb.sb(es0, "kmemT", [128, 4, 256], BF16)
        vmem = kb.sb(es0, "vmem", [128, 2, 512], BF16)
        PSP = [kb.ps(es0, "psp%d" % i, [128, 1024], F32) for i in range(4)]
        PS = [PSP[i // 2][:, (i % 2) * 512:(i % 2 + 1) * 512] for i in range(8)]
        PSB = PS[7][:, :].bitcast(BF16)
        psrr = [0]

        def nps():
            i = psrr[0]
            psrr[0] = (i + 1) % 4
            return PS[i], "PS%d" % i

        evrr = [0]

        def evac_eng():
            evrr[0] ^= 1
            return "dve" if evrr[0] else "act"

        def copy_op(eng, R, W, out_ap, in_ap):
            if eng == "act":
                op("act", R, W, lambda e: e.copy(out=out_ap, in_=in_ap))
            else:
                op(eng, R, W, lambda e: e.tensor_copy(out=out_ap, in_=in_ap))

        def dump(name, R, src_ap, dst_ap=None):
            if name in dbg:
                dma("sp" if src_ap.dtype == F32 else "pool", R, ["dbgout"], dbg[name] if dst_ap is None else dst_ap, src_ap)

        def rstd_from_sum(es, tag, ssum_ap, R, n, dim, Wname):
            shape = list(ssum_ap.shape)
            t = kb.sb(es, "rs_" + tag, shape, F32)
            op("dve", R, [Wname], lambda e: e.tensor_scalar(out=t[:], in0=ssum_ap, scalar1=1.0 / dim, scalar2=EPS, op0=ALU.mult, op1=ALU.add))
            op("act", [Wname], [Wname], lambda e: e.activation(out=t[:], in_=t[:], func=AF.Sqrt))
            op("dve", [Wname], [Wname], lambda e: e.reciprocal(out=t[:], in_=t[:]))
            return t

        dma("sp", [], ["ident_f"], ident_f[:], cident)
        dma("pool", [], ["ident_b"], ident_b[:], cident)
        dma("sp", [], ["ccols"], ccols[:], ccol)
        dma("sp", [], ["gpre"], gpre[:], col_view(pre_norm, 8), allow_slow_non_contiguous=True)
        op("dve", [], ["ones_b"], lambda e: e.memset(ones_b[:], 1.0))

        esT = ExitStack()
        ssm_T = ssm_tables_alloc(kb, esT)
        esA = ExitStack()
        cosT = kb.sb(esA, "cosT", [128, L // 4], F32)
        sinT = kb.sb(esA, "sinT", [128, L // 4], F32)

        def rope_tab(tab, c):
            blk = c // 2
            return tab[32 * blk:32 * blk + 32, (c % 2) * 512:(c % 2) * 512 + 512]

        NA = 1088
        wa = kb.sb(esA, "wa", [128, 8, NA], BF16)
        gq = kb.sb(esA, "gq", [128, 2], F32)
        gkv = kb.sb(esA, "gkv", [128, 1], F32)
        wq = kb.sb(esA, "wq", [128, 2, 8, 128], BF16)
        wkv2 = kb.sb(esA, "wkv2", [128, 1024], BF16)
        wqv = w_q_up.rearrange("(k p) (h c) -> p k h c", p=128, c=96)

        def issue_early_weight_loads():
            dma("pool", [], ["wa"], wa[:, :, 0:512], w_in_v[:, :, O_U:O_U + 512])
            dma("pool", [], ["wa"], wa[:, :, 512:768], w_in_v[:, :, O_CQ:O_CQ + 256])
            dma("pool", [], ["wa"], wa[:, :, 768:896], w_in_v[:, :, O_CKV:O_CKV + 128])
            dma("pool", [], ["wa"], wa[:, :, 896:960], w_in_v[:, :, O_CKV:O_CKV + 64])
            dma("pool", [], ["wa"], wa[:, :, 960:992], w_in_v[:, :, O_KR:O_KR + 32])
            dma("pool", [], ["wa"], wa[:, :, 992:1056], w_in_v[:, :, O_CKV:O_CKV + 64])
            dma("pool", [], ["wa"], wa[:, :, 1056:1072], w_in_v[:, :, O_KR + 16:O_KR + 32])
            dma("pool", [], ["wa"], wa[:, :, 1072:1088], w_in_v[:, :, O_KR:O_KR + 16])
            dma("sp", [], ["gq"], gq[:], col_view(q_norm, 2), allow_slow_non_contiguous=True)
            dma("sp", [], ["gkv"], gkv[:], col_view(kv_norm, 1), allow_slow_non_contiguous=True)
            for k in range(2):
                dma("pool", [], ["wq"], wq[:, k, :, 0:96], wqv[:, k, :, :])
                dma("pool", [], ["wq"], wq[:, k, :, 96:112], wqv[:, k, :, 80:96])
                dma("pool", [], ["wq"], wq[:, k, :, 112:128], wqv[:, k, :, 64:80])
            dma("pool", [], ["wkv2"], wkv2[:], w_kv_up)

        with ExitStack() as es:
            RC = L // 4
            pos_i = kb.sb(es, "pos_i", [128, RC], I32)
            ang = kb.sb(es, "ang", [128, RC], F32)
            t1 = kb.sb(es, "t1", [128, RC], F32)
            ki = kb.sb(es, "ki", [128, RC], I32)
            for blk in range(4):
                dma("sp", [], ["pos_i"], pos_i[32 * blk:32 * blk + 32, :], pos[:, blk * RC:(blk + 1) * RC].partition_broadcast(32))
            op("dve", ["pos_i"], ["ang"], lambda e: e.tensor_copy(out=ang[:], in_=pos_i[:]))
            op("dve", ["ang", "ccols"], ["ang"], lambda e: e.tensor_scalar(out=ang[:], in0=ang[:], scalar1=ccols[:, 0:1], scalar2=None, op0=ALU.mult))
            for which, tab, tn, scale_ap in (("sin", sinT, "sinT", ccols[:, 1:2]), ("cos", cosT, "cosT", 1.0)):
                if which == "cos":
                    op("dve", ["ang"], ["ang"], lambda e: e.tensor_scalar(out=ang[:], in0=ang[:], scalar1=math.pi / 2, scalar2=None, op0=ALU.add))
                op("dve", ["ang"], ["t1"], lambda e: e.tensor_scalar(out=t1[:], in0=ang[:], scalar1=1.0 / TWO_PI, scalar2=None, op0=ALU.mult))
                op("dve", ["t1"], ["ki"], lambda e: e.tensor_copy(out=ki[:], in_=t1[:]))
                op("dve", ["ki"], ["t1"], lambda e: e.tensor_copy(out=t1[:], in_=ki[:]))
                op("dve", ["t1", "ang"], [tn], lambda e: e.scalar_tensor_tensor(out=tab[:], in0=t1[:], scalar=-C1, in1=ang[:], op0=ALU.mult, op1=ALU.add))
                op("dve", ["t1", tn], [tn], lambda e: e.scalar_tensor_tensor(out=tab[:], in0=t1[:], scalar=-C2, in1=tab[:], op0=ALU.mult, op1=ALU.add))
                op("dve", [tn], [tn], lambda e: e.tensor_scalar(out=tab[:], in0=tab[:], scalar1=-math.pi, scalar2=math.pi, op0=ALU.max, op1=ALU.min))
                op("act", [tn, "ccols"], [tn], lambda e: e.activation(out=tab[:], in_=tab[:], func=AF.Sin, scale=scale_ap))
            dump("cosT", ["cosT"], cosT[:])
            dump("sinT", ["sinT"], sinT[:])

            memt = kb.sb(es, "memt", [128, 2, D], F32)
            memn = kb.sb(es, "memn", [128, 2, D], BF16)
            gmem = kb.sb(es, "gmem", [128, D], F32)
            memnT = kb.sb(es, "memnT", [128, 8, 256], BF16)
            wkv = kb.sb(es, "wkv", [128, 8, 1024], BF16)
            junk = kb.sb(es, "junk", [128, D], F32)
            ssm_ = kb.sb(es, "ssm_", [128, 2], F32)
            dma("sp", [], ["memt"], memt[:], mem.rearrange("(t p) d -> p t d", p=128))
            dma("sp", [], ["gmem"], gmem[:], mem_norm.partition_broadcast(128))
            dma("pool", [], ["wkv"], wkv[:], mem_w_kv.rearrange("(k p) c -> p k c", p=128))
            issue_early_weight_loads()
            for t in range(2):
                op("act", ["memt"], ["junk", "ssm_"], lambda e: e.activation(out=junk[:], in_=memt[:, t, :], func=AF.Square, accum_out=ssm_[:, t:t + 1]))
            rs = rstd_from_sum(es, "mem", ssm_[:], ["ssm_"], 128, D, "rs_mem")
            for t in range(2):
                op("dve", ["memt", "rs_mem", "gmem"], ["memn"], lambda e: e.scalar_tensor_tensor(out=memn[:, t, :], in0=memt[:, t, :], scalar=rs[:, t:t + 1], in1=gmem[:], op0=ALU.mult, op1=ALU.mult))
            for t in range(2):
                for k in range(8):
                    op("pe", ["memn", "ident_b"], ["PSB"], lambda e: e.transpose(PSB[:, k * 128:(k + 1) * 128], memn[:, t, k * 128:(k + 1) * 128], ident_b[:]))
                op("dve", ["PSB"], ["memnT"], lambda e: e.tensor_copy(out=memnT[:, :, t * 128:(t + 1) * 128], in_=PSB.rearrange("p (k c) -> p k c", k=8)))
            for h in range(4):
                p_, pn = nps()

                def f(e):
                    for k in range(8):
                        i_ = e.matmul(p_[:, 0:256], lhsT=wkv[:, k, h * 128:(h + 1) * 128], rhs=memnT[:, k, :], start=(k == 0), stop=(k == 7))
                    return i_
                op("pe", ["wkv", "memnT"], [pn], f)
                op("dve", [pn], ["kmemT"], lambda e: e.tensor_copy(out=kmemT[:, h, :], in_=p_[:, 0:256]))
            for t in range(2):
                p_, pn = nps()

                def f(e):
                    for k in range(8):
                        i_ = e.matmul(p_[:, :], lhsT=memnT[:, k, t * 128:(t + 1) * 128], rhs=wkv[:, k, 512:1024], start=(k == 0), stop=(k == 7))
                    return i_
                op("pe", ["wkv", "memnT"], [pn], f)
                op("dve", [pn], ["vmem"], lambda e: e.tensor_copy(out=vmem[:, t, :], in_=p_[:, :]))
            dump("kmemT", ["kmemT"], kmemT[:])
            dump("vmem", ["vmem"], vmem[:])
            ssm_tables_ = ssm_tables(kb, ssm_T, es, locals())
            kb.barrier()

        cqnT = kb.sb(esA, "cqnT", [128, 2, L], BF16)
        ckvnT = kb.sb(esA, "ckvnT", [128, L], BF16)
        krT = kb.sb(esA, "krT", [128, L], BF16)

        def hT_norm(c, i, xts, hbs, ssx, junk, junkn):
            ti = c * 4 + i
            xt, xn = xts[ti % len(xts)], "xt%d" % (ti % len(xts))
            hb, hbn = hbs[ti % len(hbs)], "hb%d" % (ti % len(hbs))
            sl = ti % len(hbs)
            sx, sxn = ssx[:, sl, :], "ssx%d" % sl
            dma("sp", [], [xn], xt[:], x[ti * 128:(ti + 1) * 128, :])
            op("act", [xn], [junkn, sxn], lambda e: e.activation(out=junk[:], in_=xt[:], func=AF.Square, accum_out=sx[:, 0:1]))
            op("dve", [sxn], [sxn], lambda e: e.tensor_scalar(out=sx[:, 1:2], in0=sx[:, 0:1], scalar1=1.0 / D, scalar2=EPS, op0=ALU.mult, op1=ALU.add))
            op("act", [sxn], [sxn], lambda e: e.activation(out=sx[:, 2:3], in_=sx[:, 1:2], func=AF.Sqrt))
            op("dve", [sxn], [sxn], lambda e: e.reciprocal(out=sx[:, 3:4], in_=sx[:, 2:3]))
            op("dve", [xn, sxn], [hbn], lambda e: e.tensor_scalar(out=hb[:], in0=xt[:], scalar1=sx[:, 3:4], scalar2=None, op0=ALU.mult))

        def hT_tr(c, i, hT, hn, hbs):
            ti = c * 4 + i
            hb, hbn = hbs[ti % len(hbs)], "hb%d" % (ti % len(hbs))

            def f(e):
                for k in range(8):
                    i_ = e.transpose(PSB[:, k * 128:(k + 1) * 128], hb[:, k * 128:(k + 1) * 128], ident_b[:])
                return i_
            op("pe", [hbn, "ident_b"], ["PSB"], f)
            op("dve", ["PSB", "gpre"], [hn], lambda e: e.tensor_tensor(out=hT[:, :, i * 128:(i + 1) * 128], in0=PSB.rearrange("p (k c) -> p k c", k=8), in1=gpre[:].unsqueeze(2).to_broadcast([128, 8, 128]), op=ALU.mult))

        if "A" in phases:
            with ExitStack() as es:
                xts = [kb.sb(es, "xt%d" % i, [128, D], F32) for i in range(2)]
                junkb = kb.sb(es, "junkb", [128, D], BF16)
                hbs = [kb.sb(es, "hb%d" % i, [128, D], BF16) for i in range(4)]
                ssx = kb.sb(es, "ssx", [128, 4, 4], F32)
                hTs = [kb.sb(es, "hT%d" % i, [128, 8, 512], BF16) for i in range(2)]
                for i in range(4):
                    hT_norm(0, i, xts, hbs, ssx, junkb, "junkb")
                sq = kb.sb(es, "sq", [128, 3, 512], BF16)
                epsc = kb.sb(es, "epsc", [128, 1], F32)
                op("dve", [], ["epsc"], lambda e: e.memset(epsc[:], EPS))
                rq = kb.sb(es, "rq", [128, 512], F32)
                rkv = kb.sb(es, "rkv", [128, 512], F32)
                tm1 = kb.sb(es, "tm1", [128, 512], F32)
                tm2 = kb.sb(es, "tm2", [128, 512], F32)
                for c in range(NQ):
                    hT = hTs[c % 2]
                    hn = "hT%d" % (c % 2)
                    cs = slice(c * 512, (c + 1) * 512)
                    if c == 0:
                        for i in range(4):
                            hT_tr(0, i, hT, hn, hbs)
                    if c + 1 < NQ:
                        for i in range(4):
                            hT_norm(c + 1, i, xts, hbs, ssx, junkb, "junkb")

                    def proj(p_, pn, c0, m):
                        def f(e):
                            for k in range(8):
                                i_ = e.matmul(p_[0:m, :], lhsT=wa[:, k, c0:c0 + m], rhs=hT[:, k, :], start=(k == 0), stop=(k == 7))
                            return i_
                        op("pe", ["wa", hn], [pn], f)
                    for m in range(4):
                        p_, pn = nps()
                        proj(p_, pn, m * 128, 128)
                        copy_op(evac_eng(), [pn], ["uT%d_%d" % (m, c)], uT[:, m, cs], p_[:, :])
                    for m in range(2):
                        proj(PS[4 + m], "PS%d" % (4 + m), 512 + m * 128, 128)
                    pkv, pkvn = nps()
                    proj(pkv, pkvn, 768, 128)
                    pka, pkan = nps()
                    proj(pka, pkan, 896, 96)
                    pkb, pkbn = nps()
                    proj(pkb, pkbn, 992, 96)
                    for m in range(2):
                        op("act", ["PS%d" % (4 + m)], ["sq%d" % m], lambda e: e.activation(out=sq[:, m, :], in_=PS[4 + m][:, :], func=AF.Square))
                    op("act", [pkvn], ["sq2"], lambda e: e.activation(out=sq[:, 2, :], in_=pkv[:, :], func=AF.Square))

                    def f(e):
                        for m in range(2):
                            i_ = e.matmul(PS[6][:, :], lhsT=ones_b[:], rhs=sq[:, m, :], start=(m == 0), stop=(m == 1))
                        return i_
                    op("pe", ["sq0", "sq1", "ones_b"], ["PS6"], f)
                    pss, pssn = nps()
                    op("pe", ["sq2", "ones_b"], [pssn], lambda e: e.matmul(pss[:, :], lhsT=ones_b[:], rhs=sq[:, 2, :], start=True, stop=True))
                    if c + 1 < NQ:
                        for i in range(4):
                            hT_tr(c + 1, i, hTs[(c + 1) % 2], "hT%d" % ((c + 1) % 2), hbs)
                    op("act", ["PS6", "epsc"], ["rq"], lambda e: e.activation(out=rq[:], in_=PS[6][:, :], func=AF.Ln, scale=1.0 / 256, bias=epsc[:, 0:1]))
                    op("act", [pssn, "epsc"], ["rkv"], lambda e: e.activation(out=rkv[:], in_=pss[:, :], func=AF.Ln, scale=1.0 / 128, bias=epsc[:, 0:1]))
                    op("act", ["rq"], ["rq"], lambda e: e.activation(out=rq[:], in_=rq[:], func=AF.Exp, scale=-0.5))
                    op("act", ["rkv"], ["rkv"], lambda e: e.activation(out=rkv[:], in_=rkv[:], func=AF.Exp, scale=-0.5))
                    for m in range(2):
                        op("dve", ["PS%d" % (4 + m), "rq", "gq"], ["cqnT%d" % c], lambda e: e.scalar_tensor_tensor(out=cqnT[:, m, cs], in0=PS[4 + m][:, :], scalar=gq[:, m:m + 1], in1=rq[:], op0=ALU.mult, op1=ALU.mult))
                    op("dve", [pkvn, "rkv", "gkv"], ["ckvnT%d" % c], lambda e: e.scalar_tensor_tensor(out=ckvnT[:, cs], in0=pkv[:, :], scalar=gkv[:, 0:1], in1=rkv[:], op0=ALU.mult, op1=ALU.mult))
                    op("dve", [pkan, "cosT"], ["tm1"], lambda e: e.tensor_tensor(out=tm1[64:96, :], in0=pka[64:96, :], in1=rope_tab(cosT, c), op=ALU.mult))
                    op("dve", [pkbn, "sinT"], ["tm2"], lambda e: e.tensor_tensor(out=tm2[64:96, :], in0=pkb[64:96, :], in1=rope_tab(sinT, c), op=ALU.mult))
                    op("dve", ["tm1", "tm2"], ["krT%d" % c], lambda e: e.tensor_tensor(out=krT[64:96, cs], in0=tm1[64:96, :], in1=tm2[64:96, :], op=ALU.add))
                allc = lambda nm: ["%s%d" % (nm, c) for c in range(NQ)]
                dump("uT", ["uT%d_%d" % (m, c) for m in range(4) for c in range(NQ)], uT[:])
                dump("cqnT", allc("cqnT"), cqnT[:])
                dump("ckvnT", allc("ckvnT"), ckvnT[:])
                dump("krT", allc("krT"), krT[64:96, :])
                kb.barrier()

        if "M" in phases:
            with ExitStack() as es:
                ccols_c = [O_ZS + m * 128 for m in range(4)] + [O_ZM + m * 128 for m in range(4)] + [O_QM + m * 128 for m in range(4)] + [O_ZME + m * 128 for m in range(4)]
                ccols_c += [O_G + b_ * 1024 + m * 128 for m in range(8) for b_ in range(3)]
                for mi, c0 in enumerate(ccols_c):
                    dma("pool", [], ["wsc"], wsc[mi], w_in_v[:, :, c0:c0 + 128])
                for b_ in range(3):
                    dma("pool", [], ["wbsc"], wbsc[b_], w_br[b_].rearrange("(k p) c -> p k c", p=128))
                dma("pool", [], ["wosc"], wosc, w_out.rearrange("(k p) c -> p k c", p=128))
                wkvv = wkv2[:].rearrange("p (h c) -> p h c", c=128)
                KT = [kb.sb(es, "KT%d" % i, [128, L], BF16) for i in range(2)]
                VA = [kb.sb(es, "VA%d" % i, [128, NT, 128], BF16) for i in range(2)]
                op("dve", [], ["VA0"], lambda e: e.memset(VA[0][:, :, 64:128], 1.0))
                op("dve", [], ["VA1"], lambda e: e.memset(VA[1][:, :, 0:64], 1.0))
                QT = [kb.sb(es, "QT%d" % i, [128, 512], BF16) for i in range(2)]
                tq1 = kb.sb(es, "tq1", [128, 512], F32)
                tq2 = kb.sb(es, "tq2", [128, 512], F32)
                rinv = kb.sb(es, "rinv", [128, 512], F32)
                scale = 96.0 ** -0.5
                allA = lambda nm: ["%s%d" % (nm, c) for c in range(NQ)]
                free_slots = [0, 1, 2]

                def next_slot():
                    return free_slots.pop(0)

                def release(j):
                    free_slots.append(j)
                PTp = [kb.sb(es, "PTp%d" % i, [128, 1024], BF16) for i in range(3)]

                def gen_kv_steps(h, bank, bn):
                    par = h % 2
                    Kt, Kn = KT[par], "KT%d" % par
                    Va, Vn = VA[par], "VA%d" % par
                    steps = []
                    for c in range(NQ):
                        def st(c=c):
                            cs = slice(c * 512, (c + 1) * 512)
                            op("pe", ["wkv2", "ckvnT%d" % c], [bn], lambda e: e.matmul(bank[0:64, :], lhsT=wkvv[:, h, 0:64], rhs=ckvnT[:, cs], start=True, stop=True))
                            copy_op("dve", [bn], [Kn], Kt[0:64, cs], bank[0:64, :])
                        steps.append(st)

                    def st_kr(st0=steps[0]):
                        op("dve", allA("krT"), [Kn], lambda e: e.tensor_copy(out=Kt[64:96, :], in_=krT[64:96, :]))
                        st0()
                    steps[0] = st_kr
                    for t8 in range(4):
                        def st(t8=t8):
                            def f(e):
                                for j in range(8):
                                    t = t8 * 8 + j
                                    i_ = e.matmul(bank[:, j * 64:(j + 1) * 64], lhsT=ckvnT[:, t * 128:(t + 1) * 128], rhs=wkvv[:, h, 64:128], start=True, stop=True)
                                return i_
                            op("pe", ["wkv2"] + allA("ckvnT"), [bn], f)
                            vo = 64 if par else 0
                            copy_op("dve", [bn], [Vn], Va[:, t8 * 8:(t8 + 1) * 8, vo:vo + 64], bank[:, :].rearrange("p (j c) -> p j c", c=64))
                        steps.append(st)
                    return steps

                def gen_q(h, qg, qi):
                    qs = slice(qg * 512, (qg + 1) * 512)
                    Qt, Qn = QT[qi % 2], "QT%d" % (qi % 2)
                    qa, qan = PS[6 + qi % 2], "PS%d" % (6 + qi % 2)

                    def f(e):
                        for k in range(2):
                            i_ = e.matmul(qa[:, :], lhsT=wq[:, k, h, :], rhs=cqnT[:, k, qs], start=(k == 0), stop=(k == 1))
                        return i_
                    op("pe", ["wq", "cqnT%d" % qg], [qan], f)
                    copy_op("dve", [qan], [Qn], Qt[0:64, :], qa[0:64, :])
                    op("dve", [qan, "cosT"], ["tq1"], lambda e: e.tensor_tensor(out=tq1[64:96, :], in0=qa[64:96, :], in1=rope_tab(cosT, qg), op=ALU.mult))
                    op("dve", [qan, "sinT"], ["tq2"], lambda e: e.tensor_tensor(out=tq2[64:96, :], in0=qa[96:128, :], in1=rope_tab(sinT, qg), op=ALU.mult))
                    op("dve", ["tq1", "tq2"], [Qn], lambda e: e.tensor_tensor(out=Qt[64:96, :], in0=tq1[64:96, :], in1=tq2[64:96, :], op=ALU.add))
                    if h == 0 and qg == 0:
                        dump("QT", [Qn], Qt[0:96, :])

                items = [(h, qg) for h in range(8) for qg in range(NQ)]
                LA = 2
                NPAIR = NT // 2
                for st in gen_kv_steps(0, PS[7], "PS7"):
                    st()
                gen_q(0, 0, 0)
                kv_steps = []
                stream = [(qi, p) for qi in range(len(items)) for p in range(NPAIR)]
                slots = {}

                def score_pair(n):
                    qi, p = stream[n]
                    h, qg = items[qi]
                    par = h % 2
                    Kt, Kn = KT[par], "KT%d" % par
                    Qt, Qn = QT[qi % 2], "QT%d" % (qi % 2)
                    j = next_slot()
                    slots[n] = j

                    def f(e):
                        for i in range(2):
                            kt = 2 * p + i
                            i_ = e.matmul(PS[2 * j + i][:, :], lhsT=Kt[0:96, kt * 128:(kt + 1) * 128], rhs=Qt[0:96, :], start=True, stop=True)
                        return i_
                    op("pe", [Kn, Qn], ["PS%d" % (2 * j), "PS%d" % (2 * j + 1)], f)
                for n in range(LA):
                    score_pair(n)
                for n, (qi, p) in enumerate(stream):
                    h, qg = items[qi]
                    par = h % 2
                    Va, Vn = VA[par], "VA%d" % par
                    qs = slice(qg * 512, (qg + 1) * 512)
                    pv, pvn = PS[6 + qi % 2], "PS%d" % (6 + qi % 2)
                    if n + LA < len(stream):
                        score_pair(n + LA)
                    j = slots.pop(n)
                    pt, ptn = PTp[n % 3], "PTp%d" % (n % 3)
                    op("act", ["PS%d" % (2 * j), "PS%d" % (2 * j + 1)], [ptn], lambda e: e.activation(out=pt[:], in_=PSP[j][:, :], func=AF.Exp, scale=scale))

                    def f(e):
                        for i in range(2):
                            kt = 2 * p + i
                            i_ = e.matmul(pv[:, :], lhsT=Va[:, kt, :], rhs=pt[:, i * 512:(i + 1) * 512], start=(kt == 0), stop=(kt == NT - 1))
                        return i_
                    op("pe", [ptn, Vn], [pvn], f)
                    release(j)
                    if p == 8 and qi + 1 < len(items):
                        gen_q(items[qi + 1][0], items[qi + 1][1], qi + 1)
                    if p == 2 and qg == 3 and h + 1 < 8:
                        kv_steps = gen_kv_steps(h + 1, PS[6 + (qi + 1) % 2], "PS%d" % (6 + (qi + 1) % 2))
                    if p >= 3 and p != 8 and kv_steps:
                        kv_steps.pop(0)()
                    if p == NPAIR - 1:
                        assert not kv_steps
                        ro = slice(0, 64) if par == 0 else slice(64, 128)
                        rs_ = slice(64, 128) if par == 0 else slice(0, 64)
                        op("dve", [pvn], ["rinv"], lambda e: e.reciprocal(out=rinv[ro, :], in_=pv[rs_, :]))
                        op("dve", [pvn, "rinv"], ["ymlaT%d" % qg], lambda e: e.tensor_tensor(out=ymlaT[ro, h // 2, qs], in0=pv[ro, :], in1=rinv[ro, :], op=ALU.mult))
                dump("ymlaT", allA("ymlaT"), ymlaT[:])
                kb.barrier()
        esA.close()

        if "S" in phases:
            G_ = dict(locals())
            G_["ssm_tables"] = ssm_tables_
            ssm_phase(kb, es0, G_)
        esT.close()

        if "C" in phases:
            phase_c(kb, es0, locals())

        kb.barrier()
    return nc


def ssm_tables_alloc(kb, esT):
    jm = kb.sb(esT, "jm", [128, 128], F32)
    bd = kb.sb(esT, "bd", [128, 128], F32)
    jm_b = kb.sb(esT, "jm_b", [128, 128], BF16)
    prb16 = kb.sb(esT, "prb16", [128, 2, 9, 32], BF16)
    pib16 = kb.sb(esT, "pib16", [128, 2, 9, 32], BF16)
    P1 = kb.sb(esT, "P1", [128, 2, 9, 32], F32)
    P2 = kb.sb(esT, "P2", [128, 2, 9, 32], F32)
    Q1 = kb.sb(esT, "Q1", [128, 2, 9, 32], F32)
    Q2 = kb.sb(esT, "Q2", [128, 2, 9, 32], F32)
    zre = kb.sb(esT, "zre", [128, 2, 32], F32)
    zim = kb.sb(esT, "zim", [128, 2, 32], F32)
    return dict(P1=P1, P2=P2, Q1=Q1, Q2=Q2, zre=zre, zim=zim, jm_b=jm_b, prb16=prb16, pib16=pib16, jm=jm, bd=bd)


def ssm_tables(kb, T, es2, G):
    nc = kb.nc
    op, dma = kb.op, kb.dma
    PS, ident_f, dump = G["PS"], G["ident_f"], G["dump"]
    cjm, cbd = G["cjm"], G["cbd"]
    lam_re, lam_im, log_dt = G["lam_re"], G["lam_im"], G["log_dt"]
    NP = 17
    P1, P2, Q1, Q2, zre, zim, jm_b, prb16, pib16, jm, bd = (T[k] for k in ("P1", "P2", "Q1", "Q2", "zre", "zim", "jm_b", "prb16", "pib16", "jm", "bd"))
    dma("sp", [], ["jm"], jm[:], cjm)
    dma("sp", [], ["bd"], bd[:], cbd)
    pr = kb.sb(es2, "pr", [128, 2, NP, 32], F32)
    pi = kb.sb(es2, "pi", [128, 2, NP, 32], F32)
    lt = kb.sb(es2, "lt", [32, 2, 2, 128], F32)
    for d in range(2):
        for ri, src in enumerate((lam_re, lam_im)):
            for hlf in range(2):
                dma("sp", [], ["lt"], lt[:, d, ri, hlf * 64:(hlf + 1) * 64], src[d])
    lr = kb.sb(es2, "lr", [128, 2, 32], F32)
    li = kb.sb(es2, "li", [128, 2, 32], F32)
    dtt = kb.sb(es2, "dtt", [128, 2, 32], F32)
    for d in range(2):
        for ri, dst in enumerate((lr, li)):
            op("pe", ["lt", "ident_f"], ["PS0"], lambda e: e.matmul(PS[0][:, 0:32], lhsT=lt[:, d, ri, :], rhs=ident_f[0:32, 0:32], start=True, stop=True))
            op("dve", ["PS0"], ["lrli"], lambda e: e.tensor_copy(out=dst[:, d, :], in_=PS[0][:, 0:32]))
    dma("sp", [], ["dtt"], dtt[:].rearrange("p d g -> p (d g)"), log_dt.rearrange("(o d) g -> o (d g)", o=1).partition_broadcast(128))
    tA = kb.sb(es2, "tA", [128, 2, 32], F32)
    tB = kb.sb(es2, "tB", [128, 2, 32], F32)
    tC = kb.sb(es2, "tC", [128, 2, 32], F32)
    tI = kb.sb(es2, "tI", [128, 2, 32], I32)
    dec = kb.sb(es2, "dec", [128, 2, 32], F32)
    X = ["sprm"]
    op("act", ["dtt"], X, lambda e: e.activation(out=dtt[:], in_=dtt[:], func=AF.Exp))
    op("dve", ["lrli"] + X, X, lambda e: e.tensor_tensor(out=tA[:], in0=lr[:], in1=dtt[:], op=ALU.mult))
    op("act", X, X, lambda e: e.activation(out=dec[:], in_=tA[:], func=AF.Exp))
    op("dve", ["lrli"] + X, X, lambda e: e.tensor_tensor(out=tA[:], in0=li[:], in1=dtt[:], op=ALU.mult))

    def sin_of(dst, shift):
        op("dve", X, X, lambda e: e.tensor_scalar(out=tB[:], in0=tA[:], scalar1=shift, scalar2=1.0 / TWO_PI, op0=ALU.add, op1=ALU.mult))
        op("dve", X, X, lambda e: e.tensor_copy(out=tI[:], in_=tB[:]))
        op("dve", X, X, lambda e: e.tensor_copy(out=tB[:], in_=tI[:]))
        op("dve", X, X, lambda e: e.tensor_scalar(out=tC[:], in0=tA[:], scalar1=shift, scalar2=None, op0=ALU.add))
        op("dve", X, X, lambda e: e.scalar_tensor_tensor(out=tC[:], in0=tB[:], scalar=-C1, in1=tC[:], op0=ALU.mult, op1=ALU.add))
        op("dve", X, X, lambda e: e.scalar_tensor_tensor(out=tC[:], in0=tB[:], scalar=-C2, in1=tC[:], op0=ALU.mult, op1=ALU.add))
        op("dve", X, X, lambda e: e.tensor_scalar(out=tC[:], in0=tC[:], scalar1=-math.pi, scalar2=math.pi, op0=ALU.max, op1=ALU.min))
        op("act", X, X, lambda e: e.activation(out=dst, in_=tC[:], func=AF.Sin))
    sn = kb.sb(es2, "sn", [128, 2, 32], F32)
    cs_ = kb.sb(es2, "cs_", [128, 2, 32], F32)
    sin_of(sn[:], 0.0)
    sin_of(cs_[:], math.pi / 2)
    op("dve", X, X, lambda e: e.memset(pr[:, :, 0, :], 1.0))
    op("dve", X, X, lambda e: e.memset(pi[:, :, 0, :], 0.0))
    op("dve", X, X, lambda e: e.tensor_tensor(out=pr[:, :, 1, :], in0=dec[:], in1=cs_[:], op=ALU.mult))
    op("dve", X, X, lambda e: e.tensor_tensor(out=pi[:, :, 1, :], in0=dec[:], in1=sn[:], op=ALU.mult))

    def cmul(io, ia, ib):
        op("dve", X, X, lambda e: e.tensor_tensor(out=tA[:], in0=pr[:, :, ia, :], in1=pr[:, :, ib, :], op=ALU.mult))
        op("dve", X, X, lambda e: e.tensor_tensor(out=tB[:], in0=pi[:, :, ia, :], in1=pi[:, :, ib, :], op=ALU.mult))
        op("dve", X, X, lambda e: e.tensor_tensor(out=pr[:, :, io, :], in0=tA[:], in1=tB[:], op=ALU.subtract))
        op("dve", X, X, lambda e: e.tensor_tensor(out=tA[:], in0=pr[:, :, ia, :], in1=pi[:, :, ib, :], op=ALU.mult))
        op("dve", X, X, lambda e: e.tensor_tensor(out=tB[:], in0=pi[:, :, ia, :], in1=pr[:, :, ib, :], op=ALU.mult))
        op("dve", X, X, lambda e: e.tensor_tensor(out=pi[:, :, io, :], in0=tA[:], in1=tB[:], op=ALU.add))
    for k in range(2, 9):
        cmul(k, k - 1, 1)
    for k in range(9, NP):
        cmul(k, k - 1, k - 1)
    den = kb.sb(es2, "den", [128, 2, 32], F32)
    fre = kb.sb(es2, "fre", [128, 2, 32], F32)
    op("dve", X, X, lambda e: e.tensor_tensor(out=tA[:], in0=lr[:], in1=lr[:], op=ALU.mult))
    op("dve", X, X, lambda e: e.tensor_tensor(out=tB[:], in0=li[:], in1=li[:], op=ALU.mult))
    op("dve", X, X, lambda e: e.tensor_tensor(out=den[:], in0=tA[:], in1=tB[:], op=ALU.add))
    op("dve", X, X, lambda e: e.reciprocal(out=den[:], in_=den[:]))
    op("dve", X, X, lambda e: e.tensor_scalar(out=fre[:], in0=pr[:, :, 1, :], scalar1=-1.0, scalar2=None, op0=ALU.add))
    op("dve", X, X, lambda e: e.tensor_tensor(out=tA[:], in0=fre[:], in1=lr[:], op=ALU.mult))
    op("dve", X, X, lambda e: e.tensor_tensor(out=tB[:], in0=pi[:, :, 1, :], in1=li[:], op=ALU.mult))
    op("dve", X, X, lambda e: e.tensor_tensor(out=tA[:], in0=tA[:], in1=tB[:], op=ALU.add))
    op("dve", X, X, lambda e: e.tensor_tensor(out=zre[:], in0=tA[:], in1=den[:], op=ALU.mult))
    op("dve", X, X, lambda e: e.tensor_tensor(out=tA[:], in0=pi[:, :, 1, :], in1=lr[:], op=ALU.mult))
    op("dve", X, X, lambda e: e.tensor_tensor(out=tB[:], in0=fre[:], in1=li[:], op=ALU.mult))
    op("dve", X, X, lambda e: e.tensor_tensor(out=tA[:], in0=tA[:], in1=tB[:], op=ALU.subtract))
    op("dve", X, X, lambda e: e.tensor_tensor(out=zim[:], in0=tA[:], in1=den[:], op=ALU.mult))
    lo, hi = slice(0, 64), slice(64, 128)
    op("dve", X, X, lambda e: e.tensor_copy(out=P1[lo], in_=pr[lo, :, 0:9, :]))
    op("dve", X, X, lambda e: e.tensor_copy(out=P1[hi], in_=pi[hi, :, 0:9, :]))
    op("dve", X, X, lambda e: e.tensor_scalar(out=P2[lo], in0=pi[lo, :, 0:9, :], scalar1=-1.0, scalar2=None, op0=ALU.mult))
    op("dve", X, X, lambda e: e.tensor_copy(out=P2[hi], in_=pr[hi, :, 0:9, :]))
    op("dve", X, X, lambda e: e.tensor_copy(out=Q1[lo], in_=pr[lo, :, 0:9, :]))
    op("dve", X, X, lambda e: e.tensor_scalar(out=Q1[hi], in0=pi[hi, :, 0:9, :], scalar1=-1.0, scalar2=None, op0=ALU.mult))
    op("dve", X, X, lambda e: e.tensor_scalar(out=Q2[lo], in0=pi[lo, :, 0:9, :], scalar1=-1.0, scalar2=None, op0=ALU.mult))
    op("dve", X, X, lambda e: e.tensor_scalar(out=Q2[hi], in0=pr[hi, :, 0:9, :], scalar1=-1.0, scalar2=None, op0=ALU.mult))
    op("dve", ["jm"], ["jm_b"], lambda e: e.tensor_copy(out=jm_b[:], in_=jm[:]))
    op("dve", X, ["prb16"], lambda e: e.tensor_copy(out=prb16[:], in_=pr[:, :, 8:17, :]))
    op("dve", X, ["pib16"], lambda e: e.tensor_copy(out=pib16[:], in_=pi[:, :, 8:17, :]))
    dump("pr", X, pr[:])
    dump("pi", X, pi[:])
    dump("zre", X, zre[:])
    pass

    return dict(P1=P1, P2=P2, Q1=Q1, Q2=Q2, zre=zre, zim=zim, jm_b=jm_b, prb16=prb16, pib16=pib16, jm=jm, bd=bd)


def ssm_phase(kb, es0, G):
    nc = kb.nc
    op, dma = kb.op, kb.dma
    PS, uT, ident_f, ident_b, ccols, dump = G["PS"], G["uT"], G["ident_f"], G["ident_b"], G["ccols"], G["dump"]
    cjm, cbd = G["cjm"], G["cbd"]
    lam_re, lam_im, log_dt, b_re, b_im, c_re, c_im = (G[k] for k in ("lam_re", "lam_im", "log_dt", "b_re", "b_im", "c_re", "c_im"))
    ssm_d, glu_w, glu_b = G["ssm_d"], G["glu_w"], G["glu_b"]
    copy_op, evac_eng = G["copy_op"], G["evac_eng"]
    NP = 17
    NCH = 512
    with ExitStack() as es:
        dcol = kb.sb(es, "dcol", [128, 4], F32)
        dma("sp", [], ["dcol"], dcol[:], ssm_d.rearrange("o (k p) -> p (o k)", p=128), allow_slow_non_contiguous=True)
        gbcol = kb.sb(es, "gbcol", [128, 4], F32)
        dma("sp", [], ["gbcol"], gbcol[:], glu_b.rearrange("o (k p) -> p (o k)", p=128), allow_slow_non_contiguous=True)
        yacc = kb.sb(es, "yacc", [128, L], F32)
        T = G["ssm_tables"]
        P1, P2, Q1, Q2, zre, zim, jm_b, prb16, pib16, jm, bd = (T[k] for k in ("P1", "P2", "Q1", "Q2", "zre", "zim", "jm_b", "prb16", "pib16", "jm", "bd"))
        bre = [kb.sb(es, "bre%d" % i, [128, 8, 16], F32) for i in range(2)]
        bim = [kb.sb(es, "bim%d" % i, [128, 8, 16], F32) for i in range(2)]
        ctl = [kb.sb(es, "ctl%d" % i, [128, 2, 128], F32) for i in range(2)]
        bbre = kb.sb(es, "bbre", [128, 8, 16], F32)
        bbim = kb.sb(es, "bbim", [128, 8, 16], F32)
        tb1 = kb.sb(es, "tb1", [128, 8, 16], F32)
        tb2 = kb.sb(es, "tb2", [128, 8, 16], F32)
        creD = [kb.sb(es, "creD%d" % i, [128, 8, 16], F32) for i in range(2)]
        cimD = [kb.sb(es, "cimD%d" % i, [128, 8, 16], F32) for i in range(2)]
        ATB = [kb.sb(es, "ATB%d" % i, [128, 8, 8, 16], BF16) for i in range(2)]
        tw1 = kb.sb(es, "tw1", [128, 8, 8, 16], F32)
        tw2 = kb.sb(es, "tw2", [128, 8, 8, 16], F32)
        GBm = [kb.sb(es, "GB%d" % i, [128, 8, 128], BF16) for i in range(2)]
        OT = kb.sb(es, "OT", [128, 8, 8, 128], BF16)
        cmT = [kb.sb(es, "cmT%d" % i, [128, 8, 16], BF16) for i in range(2)]
        Kb = kb.sb(es, "Kb", [128, 8, 128], BF16)
        ATs = [kb.sb(es, "AT%d" % i, [128, 8, 9, 128], BF16) for i in range(2)]
        att = kb.sb(es, "att", [128, 2, 9, 128], BF16)
        Sb = [[kb.sb(es, "S%d_%d" % (g, i), [128, NCH], BF16) for i in range(2)] for g in range(8)]
        op("pool", [], ["OT"], lambda e: e.memset(OT[:], 0.0))

        UTd = kb.sb(es, "UTd", [128, 8, NCH], BF16)
        OTd = RawAP(OT, 0, [[OT[:].ap[0][0], 128], [8 * 128 + 16, 8], [128, 8], [1, 16]])
        ident_bb = ident_b[:].unsqueeze(1).unsqueeze(1).to_broadcast([128, 2, 9, 128])
        jm_bb = jm_b[:].unsqueeze(1).unsqueeze(1).to_broadcast([128, 2, 9, 128])

        def gen_AT(idx):
            ct_, d_ = combos[idx]
            ATx = ATs[idx % 2]
            for g2 in range(4):
                gsl = slice(ct_ * 8 + g2 * 2, ct_ * 8 + g2 * 2 + 2)
                ATn = ["AT%d_%d" % (idx % 2, g) for g in range(g2 * 2, g2 * 2 + 2)]
                prb = prb16[:, d_, :, gsl].rearrange("p l g -> p g l").unsqueeze(3).to_broadcast([128, 2, 9, 128])
                pib = pib16[:, d_, :, gsl].rearrange("p l g -> p g l").unsqueeze(3).to_broadcast([128, 2, 9, 128])
                op("pool", ["jm_b", "pib16"], ["att"], lambda e: e.tensor_tensor(out=att[:], in0=jm_bb, in1=pib, op=ALU.mult))
                op("pool", ["ident_b", "prb16"], ATn, lambda e: e.tensor_tensor(out=ATx[:, g2 * 2:(g2 + 1) * 2], in0=ident_bb, in1=prb, op=ALU.mult))
                op("pool", ["att"] + ATn, ATn, lambda e: e.tensor_tensor(out=ATx[:, g2 * 2:(g2 + 1) * 2], in0=ATx[:, g2 * 2:(g2 + 1) * 2], in1=att[:], op=ALU.add))
        combos = [(ct, d) for ct in range(4) for d in range(2)]

        def load_params(idx):
            ct, d = combos[idx]
            i = idx % 2
            gs = slice(ct * 8, ct * 8 + 8)
            for hlf in range(2):
                hs = slice(hlf * 64, hlf * 64 + 64)
                dma("sp", [], ["bre%d" % i], bre[i][hs], b_re[d, gs].rearrange("g n q -> n g q"))
                dma("sp", [], ["bim%d" % i], bim[i][hs], b_im[d, gs].rearrange("g n q -> n g q"))
                dma("sp", [], ["ctl%d" % i], ctl[i][:, 0, hs], c_re[d, ct * 128:(ct + 1) * 128, :])
                dma("sp", [], ["ctl%d" % i], ctl[i][:, 1, hs], c_im[d, ct * 128:(ct + 1) * 128, :])
        def prep(idx):
            ct_, d_ = combos[idx]
            gs_ = slice(ct_ * 8, ct_ * 8 + 8)
            pb = idx % 2
            bre_, bim_, ctl_ = bre[pb], bim[pb], ctl[pb]
            bn, bin_, cn = "bre%d" % pb, "bim%d" % pb, "ctl%d" % pb
            creD_, cimD_, ATB_, cmT_ = creD[pb], cimD[pb], ATB[pb], cmT[pb]
            for ri, (dst, dn) in enumerate(((creD_, "creD%d" % pb), (cimD_, "cimD%d" % pb))):
                op("pe", [cn, "ident_f"], ["PS0"], lambda e: e.matmul(PS[0][:, 0:128], lhsT=ctl_[:, ri, :], rhs=ident_f[:], start=True, stop=True))
                op("dve", ["PS0"], [dn], lambda e: e.tensor_copy(out=dst[:].rearrange("p g q -> p (g q)"), in_=PS[0][:, 0:128]))
            zr_b = zre[:, d_, gs_].unsqueeze(2).to_broadcast([128, 8, 16])
            zi_b = zim[:, d_, gs_].unsqueeze(2).to_broadcast([128, 8, 16])
            op("dve", [bn], ["tb1"], lambda e: e.tensor_tensor(out=tb1[:], in0=bre_[:], in1=zr_b, op=ALU.mult))
            op("pool", [bin_], ["tb2"], lambda e: e.tensor_tensor(out=tb2[:], in0=bim_[:], in1=zi_b, op=ALU.mult))
            op("dve", ["tb1", "tb2"], ["bbre"], lambda e: e.tensor_tensor(out=bbre[:], in0=tb1[:], in1=tb2[:], op=ALU.subtract))
            op("dve", [bin_], ["tb1"], lambda e: e.tensor_tensor(out=tb1[:], in0=bim_[:], in1=zr_b, op=ALU.mult))
            op("pool", [bn], ["tb2"], lambda e: e.tensor_tensor(out=tb2[:], in0=bre_[:], in1=zi_b, op=ALU.mult))
            op("dve", ["tb1", "tb2"], ["bbim"], lambda e: e.tensor_tensor(out=bbim[:], in0=tb1[:], in1=tb2[:], op=ALU.add))
            sh = [128, 8, 8, 16]
            op("dve", ["bbre"], ["tw1"], lambda e: e.tensor_tensor(out=tw1[:], in0=P1[:, d_, 0:8, gs_].unsqueeze(3).to_broadcast(sh), in1=bbre[:].unsqueeze(1).to_broadcast(sh), op=ALU.mult))
            op("pool", ["bbim"], ["tw2"], lambda e: e.tensor_tensor(out=tw2[:], in0=P2[:, d_, 0:8, gs_].unsqueeze(3).to_broadcast(sh), in1=bbim[:].unsqueeze(1).to_broadcast(sh), op=ALU.mult))
            op("dve", ["tw1", "tw2"], ["ATB%d" % pb], lambda e: e.tensor_tensor(out=ATB_[:], in0=tw1[:], in1=tw2[:], op=ALU.add))
            op("dve", ["creD%d" % pb], ["cmT%d" % pb], lambda e: e.tensor_copy(out=cmT_[0:64], in_=creD_[0:64]))
            op("dve", ["cimD%d" % pb], ["cmT%d" % pb], lambda e: e.tensor_scalar(out=cmT_[64:128], in0=cimD_[64:128], scalar1=-1.0, scalar2=None, op0=ALU.mult))

        load_params(0)
        load_params(1)
        prep(0)
        gen_AT(0)
        for idx, (ct, d) in enumerate(combos):
            UT = uT[:, ct, :]
            UT3 = UTd[:]
            Un = ["UTd"]
            gs = slice(ct * 8, ct * 8 + 8)
            if d == 0:
                op("dve", ["uT%d_%d" % (ct, c) for c in range(NQ)], ["UTd"], lambda e: e.tensor_copy(out=UTd[:], in_=UT.rearrange("p (c j) -> p j c", j=8)))
            pb = idx % 2
            if idx + 2 < len(combos):
                load_params(idx + 2)
            creD_, cimD_, ATB_, cmT_ = creD[pb], cimD[pb], ATB[pb], cmT[pb]
            ATBn, cmTn, creDn, cimDn = "ATB%d" % pb, "cmT%d" % pb, "creD%d" % pb, "cimD%d" % pb
            for t4 in range(2):
                def f(e):
                    for t_ in range(4):
                        tau = t4 * 4 + t_
                        i_ = e.matmul(PS[0][:, t_ * 128:(t_ + 1) * 128], lhsT=ATB_[:, tau].rearrange("p g q -> p (g q)"), rhs=ident_b[:], start=True, stop=True)
                    return i_
                op("pe", [ATBn, "ident_b"], ["PS0"], f)
                ps3 = PS[0][:, :].rearrange("p (t c) -> p t c", c=128)
                op("dve", ["PS0", "ccols"], ["GB0"], lambda e: e.tensor_scalar(out=GBm[0][:, t4 * 4:(t4 + 1) * 4, :], in0=ps3, scalar1=ccols[:, 2:3], scalar2=None, op0=ALU.mult))
                op("dve", ["PS0", "ccols"], ["GB1"], lambda e: e.tensor_scalar(out=GBm[1][:, t4 * 4:(t4 + 1) * 4, :], in0=ps3, scalar1=ccols[:, 3:4], scalar2=None, op0=ALU.mult))

            def gen_out_weights():
                for t4 in range(2):
                    def f(e):
                        for t_ in range(4):
                            tau = t4 * 4 + t_
                            i_ = e.matmul(PS[1][:, t_ * 128:(t_ + 1) * 128], lhsT=ATB_[:, tau].rearrange("p g q -> p (g q)"), rhs=cmT_[:].rearrange("p g q -> p (g q)"), start=True, stop=True)
                        return i_
                    op("pe", [ATBn, cmTn], ["PS1"], f)
                    op("dve", ["PS1", "bd"], ["Kb"], lambda e: e.tensor_tensor(out=Kb[:, t4 * 4:(t4 + 1) * 4, :], in0=PS[1][:, :].rearrange("p (t c) -> p t c", c=128), in1=bd[:].unsqueeze(1).to_broadcast([128, 4, 128]), op=ALU.mult))
                shk = [128, 8, 8, 16]
                op("dve", [creDn], ["tw1"], lambda e: e.tensor_tensor(out=tw1[:], in0=Q1[:, d, 1:9, gs].unsqueeze(3).to_broadcast(shk), in1=creD_[:].unsqueeze(1).to_broadcast(shk), op=ALU.mult))
                op("pool", [cimDn], ["tw2"], lambda e: e.tensor_tensor(out=tw2[:], in0=Q2[:, d, 1:9, gs].unsqueeze(3).to_broadcast(shk), in1=cimD_[:].unsqueeze(1).to_broadcast(shk), op=ALU.mult))
                op("dve", ["tw1", "tw2"], ["OT"], lambda e: e.tensor_tensor(out=OTd, in0=tw1[:].rearrange("p k g q -> p g k q"), in1=tw2[:].rearrange("p k g q -> p g k q"), op=ALU.add))
            Sn = lambda g, i: "S%d_%d" % (g, i)
            for par in range(2):
                def f(e):
                    for j in range(8):
                        tau = 7 - j if d == 0 else j
                        for pair in range(4):
                            rows = slice(32 * pair, 32 * pair + 32)
                            i_ = e.matmul(PS[2 + pair][:, :], lhsT=GBm[par][rows, tau, :], rhs=UT3[rows, j, :], start=(j == 0), stop=(j == 7), tile_position=(32 * pair, 0))
                    return i_
                op("pe", ["GB%d" % par] + Un, ["PS2", "PS3", "PS4", "PS5"], f)
                for pair in range(4):
                    g = 2 * pair + par
                    copy_op(evac_eng(), ["PS%d" % (2 + pair)], [Sn(g, 0)], Sb[g][0][:], PS[2 + pair][:, :])
            gen_out_weights()
            if idx + 1 < len(combos):
                gen_AT(idx + 1)
            AT = ATs[idx % 2]
            hsrr = 0
            for lev in range(9):
                s_ = 1 << lev
                cur = lev % 2
                for g in range(8):
                    src, dst = Sb[g][cur], Sb[g][1 - cur]
                    hp, hn_ = PS[2 + hsrr % 4], "PS%d" % (2 + hsrr % 4)
                    hsrr += 1

                    def f(e):
                        e.matmul(hp[:, :], lhsT=ident_b[:], rhs=src[:], start=True, stop=False)
                        if d == 0:
                            return e.matmul(hp[:, s_:NCH], lhsT=AT[:, g, lev, :], rhs=src[:, 0:NCH - s_], start=False, stop=True)
                        return e.matmul(hp[:, 0:NCH - s_], lhsT=AT[:, g, lev, :], rhs=src[:, s_:NCH], start=False, stop=True)
                    op("pe", ["AT%d_%d" % (idx % 2, g), Sn(g, cur), "ident_b"], [hn_], f)
                    copy_op(evac_eng(), [hn_], [Sn(g, 1 - cur)], dst[:], hp[:, :])
            if ct == 0 and d == 0:
                dump("S0", ["S0_1"], Sb[0][1][:])
            if idx + 1 < len(combos):
                prep(idx + 1)
            yacc3 = yacc[:].rearrange("p (c j) -> p j c", j=8)
            for j in range(8):
                yp, yn = PS[j % 2], "PS%d" % (j % 2)
                kk = j + 1 if d == 0 else 8 - j
                jps = list(range(0, j + 1)) if d == 0 else list(range(j, 8))

                def f(e):
                    for n_, jp in enumerate(jps):
                        e.matmul(yp[:, :], lhsT=Kb[:, abs(j - jp), :], rhs=UT3[:, jp, :], start=(n_ == 0), stop=False)
                    for g in range(8):
                        if d == 0:
                            i_ = e.matmul(yp[:, 1:NCH], lhsT=OT[:, g, kk - 1, :], rhs=Sb[g][1][:, 0:NCH - 1], start=False, stop=(g == 7))
                        else:
                            i_ = e.matmul(yp[:, 0:NCH - 1], lhsT=OT[:, g, kk - 1, :], rhs=Sb[g][1][:, 1:NCH], start=False, stop=(g == 7))
                    return i_
                op("pe", ["Kb", "OT"] + Un + ["S%d_1" % g for g in range(8)], [yn], f)
                if d == 0:
                    op("dve", [yn, "dcol"] + Un, ["yacc"], lambda e: e.scalar_tensor_tensor(out=yacc3[:, j, :], in0=UT3[:, j, :], scalar=dcol[:, ct:ct + 1], in1=yp[:, :], op0=ALU.mult, op1=ALU.add))
                else:
                    op("dve", [yn, "yacc"], ["yacc"], lambda e: e.tensor_tensor(out=yacc3[:, j, :], in0=yacc3[:, j, :], in1=yp[:, :], op=ALU.add))
            if d == 0:
                continue
            if ct == 0:
                dump("yacc", ["yacc"], yacc[:])
            for c in range(NQ):
                cs = slice(c * 512, (c + 1) * 512)
                op("act", ["yacc"], ["uT%d_%d" % (ct, c)], lambda e: e.activation(out=uT[:, ct, cs], in_=yacc[:, cs], func=AF.Gelu))
        kb.barrier()
        glt = ATs[0][:, 0:2].rearrange("p g l c -> p (g l c)")[:, 0:2048].rearrange("p (m t) -> p m t", m=4)
        wglu = ATs[0][:, 2:4].rearrange("p g l c -> p (g l c)")[:, 0:2048].rearrange("p (k c) -> p k c", k=4)
        dma("pool", [], ["wglu"], wglu, glu_w.rearrange("(k p) c -> p k c", p=128))
        sgbs = [ATs[1][:, i:i + 1].rearrange("p g l c -> p (g l c)")[:, 0:1024].bitcast(F32) for i in range(2)]
        for c in range(NQ):
            cs = slice(c * 512, (c + 1) * 512)
            Uc = ["uT%d_%d" % (m, c) for m in range(4)]
            for m in range(4):
                gp, gn = PS[m % 4], "PS%d" % (m % 4)
                sgb, sgn = sgbs[m % 2], "sg%d" % (m % 2)

                def f(e):
                    for k in range(4):
                        i_ = e.matmul(gp[:, :], lhsT=wglu[:, k, m * 128:(m + 1) * 128], rhs=uT[:, k, cs], start=(k == 0), stop=(k == 3))
                    return i_
                op("pe", ["wglu"] + Uc, [gn], f)
                op("act", [gn, "gbcol"], [sgn], lambda e: e.activation(out=sgb, in_=gp[:, :], func=AF.Sigmoid, bias=gbcol[:, m:m + 1], scale=1.0))
                op("dve", [sgn] + Uc, ["glt%d" % m], lambda e: e.tensor_tensor(out=glt[:, m, :], in0=uT[:, m, cs], in1=sgb, op=ALU.mult))
            for m in range(4):
                op("pool", ["glt%d" % m], [Uc[m]], lambda e: e.tensor_copy(out=uT[:, m, cs], in_=glt[:, m, :]))
        dump("yssmT", ["uT%d_%d" % (m, c) for m in range(4) for c in range(NQ)], uT[:])
        kb.barrier()


def phase_c(kb, es0, G):
    nc = kb.nc
    op, dma = kb.op, kb.dma
    PS, PSB, uT, ymlaT, kmemT, vmem, ones_b, ident_b = (G[k] for k in ("PS", "PSB", "uT", "ymlaT", "kmemT", "vmem", "ones_b", "ident_b"))
    w_br, w_out, b_gate, post_norm, x, out, wsc, wbsc, wosc = (G[k] for k in ("w_br", "w_out", "b_gate", "post_norm", "x", "out", "wsc", "wbsc", "wosc"))
    hT_norm, hT_tr, dump, copy_op, evac_eng = G["hT_norm"], G["hT_tr"], G["dump"], G["copy_op"], G["evac_eng"]
    with ExitStack() as es:
        wbr = kb.sb(es, "wbr", [128, 3, 4, D], BF16)
        wo = kb.sb(es, "wo", [128, 8, D], BF16)
        bg = kb.sb(es, "bg", [128, 24], F32)
        dma("sp", [], ["bg"], bg[:], b_gate.rearrange("o (k p) -> p (o k)", p=128), allow_slow_non_contiguous=True)
        gpost = kb.sb(es, "gpost", [128, D], F32)
        dma("sp", [], ["gpost"], gpost[:], post_norm.partition_broadcast(128))
        xts = [kb.sb(es, "c_xt%d" % i, [128, D], F32) for i in range(2)]
        hbs = [kb.sb(es, "c_hb%d" % i, [128, D], BF16) for i in range(3)]
        junkc = kb.sb(es, "junkc", [128, D], BF16)
        ssx = kb.sb(es, "c_ssx", [128, 3, 4], F32)
        hTs = [kb.sb(es, "c_hT%d" % i, [128, 8, 512], BF16) for i in range(2)]
        wz = [kb.sb(es, "wz%d" % i, [128, 2, 8, 128], BF16) for i in range(2)]
        wg = [kb.sb(es, "wg%d" % i, [128, 8, 128], BF16) for i in range(3)]
        actT = kb.sb(es, "actT", [128, 3, 4, 512], BF16)
        qmT = kb.sb(es, "qmT", [128, 4, 512], BF16)
        ymemT = kb.sb(es, "ymemT", [128, 4, 512], BF16)
        pmTs = [kb.sb(es, "pmT%d" % i, [128, 2, 512], BF16) for i in range(2)]
        szs = [kb.sb(es, "sz%d" % i, [128, 512], BF16) for i in range(2)]
        prr = [0]

        def pbank():
            i = (0, 1, 6)[prr[0] % 3]
            prr[0] += 1
            return PS[i], "PS%d" % i
        gt = [kb.sb(es, "gt%d" % i, [128, 512], BF16) for i in range(2)]
        mt = kb.sb(es, "mt", [128, 512], F32)
        mt2 = kb.sb(es, "mt2", [128, 512], F32)
        mergedT = kb.sb(es, "mergedT", [128, 8, 512], BF16)
        xr = kb.sb(es, "xr0", [128, D], F32)
        ot = kb.sb(es, "ot0", [128, D], F32)
        sso = kb.sb(es, "sso", [128, 8], F32)
        wzc = [0]
        wgc = [0]
        mscale = 128.0 ** -0.5

        class Prefetch:
            def __init__(self, seq, bufs, names, loader):
                self.seq, self.bufs, self.names, self.loader = seq, bufs, names, loader
                self.issued = 0
                self.used = 0
                for _ in range(len(bufs) - 1):
                    self.issue()

            def issue(self):
                if self.issued < len(self.seq):
                    i = self.issued % len(self.bufs)
                    self.loader(self.seq[self.issued], self.bufs[i], self.names[i])
                    self.issued += 1

            def next(self, expect):
                assert self.seq[self.used] == expect, (self.seq[self.used], expect)
                i = self.used % len(self.bufs)
                self.used += 1
                self.issue()
                return self.bufs[i], self.names[i]

        wz_seq = [mi for _ in range(NQ) for mi in (0, 2, 4, 6, 8, 10, 12, 14)]
        wg_seq = [(m, b) for _ in range(NQ) for m in range(8) for b in range(3)]
        pf_wz = Prefetch(wz_seq, wz, ["wz0", "wz1"], lambda mi, t, n: dma("sp", ["wsc"], [n], t[:], wsc[mi:mi + 2].rearrange("m p k c -> p m k c")))
        pf_wg = Prefetch(wg_seq, wg, ["wg0", "wg1", "wg2"], lambda mb, t, n: dma("sp", ["wsc"], [n], t[:], wsc[16 + 3 * mb[0] + mb[1]]))

        def load_wz(mi):
            return pf_wz.next(mi)

        def load_wg(m, b):
            return pf_wg.next((m, b))

        hT_norm(0, 0, xts, hbs, ssx, junkc, "junkc")
        hT_norm(0, 1, xts, hbs, ssx, junkc, "junkc")
        for b in range(3):
            dma("sp", ["wbsc"], ["wbr"], wbr[:, b], wbsc[b])
        dma("sp", ["wosc"], ["wo"], wo[:], wosc)
        for i in range(4):
            hT_tr(0, i, hTs[0], "hTc0", hbs)
            if i + 2 < 4:
                hT_norm(0, i + 2, xts, hbs, ssx, junkc, "junkc")
        for c in range(NQ):
            cs = slice(c * 512, (c + 1) * 512)
            hT, hn = hTs[c % 2], "hTc%d" % (c % 2)

            def projT(p_, pn, wt, wn, mloc):
                def f(e):
                    for k in range(8):
                        i_ = e.matmul(p_[:, :], lhsT=wt[:, mloc, k, :], rhs=hT[:, k, :], start=(k == 0), stop=(k == 7))
                    return i_
                op("pe", [wn, hn], [pn], f)
            for b, (mi0, ysrc, ynm) in enumerate(((0, uT, lambda m: ["uT%d_%d" % (m, c)]), (4, ymlaT, lambda m: ["ymlaT%d" % c]))):
                for m in range(4):
                    if m % 2 == 0:
                        wt, wn = load_wz(mi0 + m)
                    p_, pn = pbank()
                    sz, szn = szs[m % 2], "sz%d" % (m % 2)
                    projT(p_, pn, wt, wn, m % 2)
                    op("act", [pn], [szn], lambda e: e.activation(out=sz[:], in_=p_[:, :], func=AF.Silu))
                    op("dve", [szn] + ynm(m), ["actT%d" % b], lambda e: e.tensor_tensor(out=actT[:, b, m, :], in0=sz[:], in1=ysrc[:, m, cs], op=ALU.mult))
            for m in range(4):
                if m % 2 == 0:
                    wt, wn = load_wz(8 + m)
                p_, pn = pbank()
                projT(p_, pn, wt, wn, m % 2)
                copy_op("dve", [pn], ["qmT%d" % m], qmT[:, m, :], p_[:, :])
            def mem_scores(h):
                pm, pmn = pmTs[h % 2], "pmT%d" % (h % 2)
                for mt_ in range(2):
                    bi = 2 + mt_ + 2 * (h % 2)
                    p_, pn = PS[bi], "PS%d" % bi
                    op("pe", ["kmemT", "qmT%d" % h], [pn], lambda e: e.matmul(p_[:, :], lhsT=kmemT[:, h, mt_ * 128:(mt_ + 1) * 128], rhs=qmT[:, h, :], start=True, stop=True))
                    op("act", [pn], [pmn], lambda e: e.activation(out=pm[:, mt_, :], in_=p_[:, :], func=AF.Exp, scale=mscale))
            mem_scores(0)
            for h in range(4):
                if h + 1 < 4:
                    mem_scores(h + 1)
                pm, pmn = pmTs[h % 2], "pmT%d" % (h % 2)
                pva, pvn = PS[6], "PS6"
                pra, prn = (PS[0], "PS0") if h % 2 == 0 else (PS[1], "PS1")

                def f(e):
                    for mt_ in range(2):
                        i_ = e.matmul(pva[:, :], lhsT=vmem[:, mt_, h * 128:(h + 1) * 128], rhs=pm[:, mt_, :], start=(mt_ == 0), stop=(mt_ == 1))
                    return i_
                op("pe", ["vmem", pmn], [pvn], f)

                def f(e):
                    for mt_ in range(2):
                        i_ = e.matmul(pra[:, :], lhsT=ones_b[:], rhs=pm[:, mt_, :], start=(mt_ == 0), stop=(mt_ == 1))
                    return i_
                op("pe", ["ones_b", pmn], [prn], f)
                op("act", [prn], ["mt"], lambda e: e.activation(out=mt[:], in_=pra[:, :], func=AF.Ln))
                op("act", ["mt"], ["mt"], lambda e: e.activation(out=mt[:], in_=mt[:], func=AF.Exp, scale=-1.0))
                op("dve", [pvn, "mt"], ["ymemT%d" % h], lambda e: e.tensor_tensor(out=ymemT[:, h, :], in0=pva[:, :], in1=mt[:], op=ALU.mult))
            if c == 0:
                dump("ymemT", ["ymemT%d" % h for h in range(4)], ymemT[:])
            for m in range(4):
                if m % 2 == 0:
                    wt, wn = load_wz(12 + m)
                p_, pn = pbank()
                sz, szn = szs[m % 2], "sz%d" % (m % 2)
                projT(p_, pn, wt, wn, m % 2)
                op("act", [pn], [szn], lambda e: e.activation(out=sz[:], in_=p_[:, :], func=AF.Silu))
                op("dve", [szn, "ymemT%d" % m], ["actT2"], lambda e: e.tensor_tensor(out=actT[:, 2, m, :], in0=sz[:], in1=ymemT[:, m, :], op=ALU.mult))
            it = 0
            for m in range(8):
                if m == 4 and c + 1 < NQ:
                    hT_norm(c + 1, 0, xts, hbs, ssx, junkc, "junkc")
                if m == 6 and c + 1 < NQ:
                    hT_norm(c + 1, 1, xts, hbs, ssx, junkc, "junkc")
                for b in range(3):
                    wgt, wgn = load_wg(m, b)
                    gp, gpn = PS[2 + it % 2], "PS%d" % (2 + it % 2)
                    bp, bpn = PS[4 + it % 2], "PS%d" % (4 + it % 2)
                    gtt, gtn = gt[it % 2], "gt%d" % (it % 2)
                    it += 1

                    def f(e):
                        for k in range(8):
                            i_ = e.matmul(gp[:, :], lhsT=wgt[:, k, :], rhs=hT[:, k, :], start=(k == 0), stop=(k == 7))
                        return i_
                    op("pe", [wgn, hn], [gpn], f)
                    op("act", [gpn, "bg"], [gtn], lambda e: e.activation(out=gtt[:], in_=gp[:, :], func=AF.Sigmoid, bias=bg[:, b * 8 + m:b * 8 + m + 1], scale=1.0))

                    def f(e):
                        for k in range(4):
                            i_ = e.matmul(bp[:, :], lhsT=wbr[:, b, k, m * 128:(m + 1) * 128], rhs=actT[:, b, k, :], start=(k == 0), stop=(k == 3))
                        return i_
                    op("pe", ["wbr", "actT%d" % b], [bpn], f)
                    if b == 0:
                        op("dve", [gtn, bpn], ["mt"], lambda e: e.tensor_tensor(out=mt[:], in0=gtt[:], in1=bp[:, :], op=ALU.mult))
                    else:
                        op("dve", [gtn, bpn], ["mt2"], lambda e: e.tensor_tensor(out=mt2[:], in0=gtt[:], in1=bp[:, :], op=ALU.mult))
                        if b == 1:
                            op("pool", ["mt", "mt2"], ["mt"], lambda e: e.tensor_tensor(out=mt[:], in0=mt[:], in1=mt2[:], op=ALU.add))
                        else:
                            op("pool", ["mt", "mt2"], ["mergedT"], lambda e: e.tensor_tensor(out=mergedT[:, m, :], in0=mt[:], in1=mt2[:], op=ALU.add))
            if c == 0:
                dump("mergedT", ["mergedT"], mergedT[:])
            for i in range(4):
                ti = c * 4 + i
                if c + 1 < NQ:
                    hT_tr(c + 1, i, hTs[(c + 1) % 2], "hTc%d" % ((c + 1) % 2), hbs)
                if c + 1 < NQ and i + 2 < 4:
                    hT_norm(c + 1, i + 2, xts, hbs, ssx, junkc, "junkc")
                dma("sp", [], ["xr0"], xr[:], x[ti * 128:(ti + 1) * 128, :])
                banks = (0, 1) if i % 2 == 0 else (6, 3)
                for half in range(2):
                    p_, pn = PS[banks[half]], "PS%d" % banks[half]

                    def f(e):
                        for k in range(8):
                            i_ = e.matmul(p_[:, :], lhsT=mergedT[:, k, i * 128:(i + 1) * 128], rhs=wo[:, k, half * 512:(half + 1) * 512], start=(k == 0), stop=(k == 7))
                        return i_
                    op("pe", ["mergedT", "wo"], [pn], f)
                    sl = (i % 2) * 4 + half
                    op("act", [pn], ["junkc", "sso%d" % (i % 2)], lambda e: e.activation(out=junkc[:, half * 512:(half + 1) * 512], in_=p_[:, :], func=AF.Square, accum_out=sso[:, sl:sl + 1]))
                so = (i % 2) * 4
                sn_ = "sso%d" % (i % 2)
                op("dve", [sn_], [sn_], lambda e: e.tensor_tensor(out=sso[:, so + 2:so + 3], in0=sso[:, so:so + 1], in1=sso[:, so + 1:so + 2], op=ALU.add))
                op("dve", [sn_], [sn_], lambda e: e.tensor_scalar(out=sso[:, so + 2:so + 3], in0=sso[:, so + 2:so + 3], scalar1=1.0 / D, scalar2=EPS, op0=ALU.mult, op1=ALU.add))
                op("act", [sn_], [sn_], lambda e: e.activation(out=sso[:, so + 2:so + 3], in_=sso[:, so + 2:so + 3], func=AF.Sqrt))
                op("dve", [sn_], [sn_], lambda e: e.reciprocal(out=sso[:, so + 3:so + 4], in_=sso[:, so + 2:so + 3]))
                for half in range(2):
                    hs = slice(half * 512, (half + 1) * 512)
                    op("dve", ["PS%d" % banks[half], sn_, "gpost"], ["ot0"], lambda e: e.scalar_tensor_tensor(out=ot[:, hs], in0=PS[banks[half]][:, :], scalar=sso[:, so + 3:so + 4], in1=gpost[:, hs], op0=ALU.mult, op1=ALU.mult))
                op("pool", ["ot0", "xr0"], ["ot0"], lambda e: e.tensor_tensor(out=ot[:], in0=ot[:], in1=xr[:], op=ALU.add))
                dma("pool", ["ot0"], ["outdram"], out[ti * 128:(ti + 1) * 128, :], ot[:])
        kb.barrier()


def host_consts():
    ident = np.eye(128, dtype=np.float32)
    jm = np.zeros((128, 128), np.float32)
    r = np.arange(64)
    jm[r, 64 + r] = 1.0
    jm[64 + r, r] = -1.0
    bd = np.kron(np.eye(8, dtype=np.float32), np.ones((16, 16), np.float32))
    col = np.zeros((128, 8), np.float32)
    p = np.arange(128)
    col[:, 0] = (10000.0 ** (-(2.0 * (p % 16)) / 32.0)).astype(np.float32)
    col[:, 1] = np.where((p % 32) < 16, -1.0, 1.0)
    col[:, 2] = ((p // 16) % 2 == 0)
    col[:, 3] = ((p // 16) % 2 == 1)
    return ident, jm, bd, col


def make_in_maps(inputs, cores):
    ident, jm, bd, col = host_consts()
    f = lambda a: np.ascontiguousarray(np.asarray(a, dtype=np.float32))
    shared = {
        "pre_norm": f(inputs["pre_norm"][0]).reshape(1, D), "w_in": f(inputs["w_in"][0]),
        "b_gate": f(inputs["b_gate"][0]).reshape(1, 3072),
        "lam_re": f(inputs["ssm_lambda_re"][0]), "lam_im": f(inputs["ssm_lambda_im"][0]), "log_dt": f(inputs["ssm_log_dt"][0]),
        "b_re": f(inputs["ssm_b_re"][0]), "b_im": f(inputs["ssm_b_im"][0]),
        "c_re": f(inputs["ssm_c_re"][0]).reshape(2, 512, 64), "c_im": f(inputs["ssm_c_im"][0]).reshape(2, 512, 64),
        "ssm_d": f(inputs["ssm_d"][0]).reshape(1, 512), "glu_w": f(inputs["ssm_glu_w"][0]), "glu_b": f(inputs["ssm_glu_b"][0]).reshape(1, 512),
        "q_norm": f(inputs["mla_q_norm"][0]).reshape(1, 256), "w_q_up": f(inputs["mla_w_q_up"][0]),
        "kv_norm": f(inputs["mla_kv_norm"][0]).reshape(1, 128), "w_kv_up": f(inputs["mla_w_kv_up"][0]),
        "mem_norm": f(inputs["mem_norm"][0]).reshape(1, D), "mem_w_kv": f(inputs["mem_w_kv"][0]),
        "w_br0": f(inputs["w_branch_ssm"][0]), "w_br1": f(inputs["w_branch_mla"][0]), "w_br2": f(inputs["w_branch_mem"][0]),
        "w_out": f(inputs["w_out"][0]), "post_norm": f(inputs["post_norm"][0]).reshape(1, D),
        "cident": ident, "cjm": jm, "cbd": bd, "ccol": col,
    }
    maps = []
    for b in cores:
        m = dict(shared)
        m["x"] = f(inputs["x"][b])
        m["mem"] = f(inputs["mem"][b])
        m["pos"] = np.ascontiguousarray(np.asarray(inputs["positions"][b], dtype=np.int32)).reshape(1, L)
        maps.append(m)
    return maps


def kernel(**inputs):
    nc = build()
    maps = make_in_maps(inputs, list(range(8)))
    res = run_bass_kernel_spmd(nc, maps, core_ids=list(range(8)))
    return np.stack([np.asarray(r["out"], dtype=np.float32) for r in res.results], axis=0)
```
